# Optimizing a Trainium2 kernel written in Bass

```python
import jax, jax.numpy as jnp
from jax import lax
import numpy as np

D_MODEL = 1024
BATCH = 4
SEQ = 4096
DEPTH = 2

CHUNK = 64
N_A_LAYERS = DEPTH // 2
N_B_LAYERS = DEPTH - N_A_LAYERS
A_WIDTH = 2 * D_MODEL
A_GROUP_LEN = 128
A_HEAD_CH = 128
A_HEADS = A_WIDTH // A_HEAD_CH
B_HEAD_DIM = 64
B_HEADS = D_MODEL // B_HEAD_DIM
B_WIDTH = B_HEADS * B_HEAD_DIM
B_PREV_CHUNKS = 8
B_BAND = (B_PREV_CHUNKS + 1) * CHUNK
B_LEFT_PAD = B_PREV_CHUNKS * CHUNK
REL_CLIP = 256
N_REL = 2 * REL_CLIP + 1
EPS = 1e-6
NEG_INF = -1e30

kernel_name = "yoco_gmlp_chunkattn_hybrid"


def rms_norm(x, g):
    xf = x.astype(jnp.float32)
    y = xf * lax.rsqrt(jnp.mean(xf * xf, axis=-1, keepdims=True) + EPS)
    return (y * g.astype(jnp.float32)).astype(x.dtype)


def layer_norm(x, g, b):
    xf = x.astype(jnp.float32)
    mu = jnp.mean(xf, axis=-1, keepdims=True)
    xc = xf - mu
    y = xc * lax.rsqrt(jnp.mean(xc * xc, axis=-1, keepdims=True) + EPS)
    return (y * g.astype(jnp.float32) + b.astype(jnp.float32)).astype(x.dtype)


def gmlp_mixer(h, w_in, ln_g, ln_b, w_s, b_s, w_out):
    bsz, s, _ = h.shape
    u, v, z = jnp.split(h @ w_in, 3, axis=-1)
    u = jax.nn.gelu(u)
    v = layer_norm(jax.nn.gelu(v), ln_g, ln_b)
    vg = v.reshape(bsz, s // A_GROUP_LEN, A_GROUP_LEN, A_HEADS, A_HEAD_CH)
    pos_chunk = jnp.arange(A_GROUP_LEN) // CHUNK
    allowed = pos_chunk[:, None] >= pos_chunk[None, :]
    w_m = jnp.where(allowed[None], w_s, jnp.zeros_like(w_s)).astype(vg.dtype)
    mixed = jnp.einsum('hts,bgshc->bgthc', w_m, vg)
    mixed = mixed + b_s.T.astype(vg.dtype)[:, :, None]
    mixed = mixed.reshape(bsz, s, A_WIDTH)
    y = u * mixed * jax.nn.silu(z)
    return y @ w_out


def shared_kv(x, g_kv, w_kv):
    bsz, s, _ = x.shape
    k, v = jnp.split(rms_norm(x, g_kv) @ w_kv, 2, axis=-1)
    pad = ((0, 0), (B_LEFT_PAD, 0), (0, 0), (0, 0))
    k = jnp.pad(k.reshape(bsz, s, B_HEADS, B_HEAD_DIM), pad)
    v = jnp.pad(v.reshape(bsz, s, B_HEADS, B_HEAD_DIM), pad)
    return k, v


def chunk_attention_mixer(h, k_pad, v_pad, w_qz, rel_bias, w_out):
    bsz, s, _ = h.shape
    n_chunks = s // CHUNK
    q, z = jnp.split(h @ w_qz, 2, axis=-1)
    scale = B_HEAD_DIM ** -0.5
    q = (q * scale).reshape(bsz, n_chunks, CHUNK, B_HEADS, B_HEAD_DIM)
    q = jnp.moveaxis(q, 1, 0)
    qi = jnp.arange(CHUNK)
    m = jnp.arange(B_BAND)
    rel = qi[:, None] - m[None, :] + B_LEFT_PAD
    idx = jnp.clip(rel, -REL_CLIP, REL_CLIP) + REL_CLIP
    bias = rel_bias[:, idx].astype(jnp.float32)

    def one_chunk(args):
        c, qc = args
        start = c * CHUNK
        kc = lax.dynamic_slice_in_dim(k_pad, start, B_BAND, axis=1)
        vc = lax.dynamic_slice_in_dim(v_pad, start, B_BAND, axis=1)
        sc = jnp.einsum('bqhd,bkhd->bhqk', qc, kc).astype(jnp.float32) + bias
        valid = (m + start) >= B_LEFT_PAD
        sc = jnp.where(valid[None, None, None, :], sc, NEG_INF)
        p = jax.nn.softmax(sc, axis=-1).astype(vc.dtype)
        return jnp.einsum('bhqk,bkhd->bqhd', p, vc)

    o = lax.map(one_chunk, (jnp.arange(n_chunks), q))
    o = jnp.moveaxis(o, 0, 1).reshape(bsz, s, B_WIDTH)
    return (o * jax.nn.silu(z)) @ w_out


def setup_inputs(seed: int = 0) -> dict:
    key = jax.random.key(seed)
    ks = jax.random.split(key, 20)
    f32 = jnp.float32
    nrm = lambda k, shp, sc: (jax.random.normal(k, shp, f32) * sc)
    x = jax.random.normal(ks[0], (BATCH, SEQ, D_MODEL), f32)
    a_norm_g = 1.0 + nrm(ks[1], (N_A_LAYERS, D_MODEL), 0.02)
    a_w_in = nrm(ks[2], (N_A_LAYERS, D_MODEL, 3 * A_WIDTH), D_MODEL ** -0.5)
    a_ln_g = 1.0 + nrm(ks[3], (N_A_LAYERS, A_WIDTH), 0.02)
    a_ln_b = nrm(ks[4], (N_A_LAYERS, A_WIDTH), 0.02)
    a_w_s = nrm(ks[5], (N_A_LAYERS, A_HEADS, A_GROUP_LEN, A_GROUP_LEN), A_GROUP_LEN ** -0.5)
    a_b_s = 1.0 + nrm(ks[6], (N_A_LAYERS, A_HEADS, A_GROUP_LEN), 0.1)
    a_w_out = nrm(ks[7], (N_A_LAYERS, A_WIDTH, D_MODEL), A_WIDTH ** -0.5)
    kv_norm_g = 1.0 + nrm(ks[8], (D_MODEL,), 0.02)
    w_kv = nrm(ks[9], (D_MODEL, 2 * B_WIDTH), D_MODEL ** -0.5)
    b_norm_g = 1.0 + nrm(ks[10], (N_B_LAYERS, D_MODEL), 0.02)
    b_w_qz = nrm(ks[11], (N_B_LAYERS, D_MODEL, 2 * B_WIDTH), D_MODEL ** -0.5)
    b_rel_bias = nrm(ks[12], (N_B_LAYERS, B_HEADS, N_REL), 0.5)
    b_w_out = nrm(ks[13], (N_B_LAYERS, B_WIDTH, D_MODEL), B_WIDTH ** -0.5)
    final_norm_g = 1.0 + nrm(ks[14], (D_MODEL,), 0.02)
    return {"x": x, "a_norm_g": a_norm_g, "a_w_in": a_w_in, "a_ln_g": a_ln_g,
            "a_ln_b": a_ln_b, "a_w_s": a_w_s, "a_b_s": a_b_s, "a_w_out": a_w_out,
            "kv_norm_g": kv_norm_g, "w_kv": w_kv, "b_norm_g": b_norm_g,
            "b_w_qz": b_w_qz, "b_rel_bias": b_rel_bias, "b_w_out": b_w_out,
            "final_norm_g": final_norm_g}


def reference(x, a_norm_g, a_w_in, a_ln_g, a_ln_b, a_w_s, a_b_s, a_w_out,
              kv_norm_g, w_kv, b_norm_g, b_w_qz, b_rel_bias, b_w_out, final_norm_g):
    k_pad = None
    v_pad = None
    for layer in range(DEPTH):
        if layer < N_A_LAYERS:
            i = layer
            h = rms_norm(x, a_norm_g[i])
            x = x + gmlp_mixer(h, a_w_in[i], a_ln_g[i], a_ln_b[i], a_w_s[i],
                               a_b_s[i], a_w_out[i])
        else:
            i = layer - N_A_LAYERS
            if i == 0:
                k_pad, v_pad = shared_kv(x, kv_norm_g, w_kv)
            h = rms_norm(x, b_norm_g[i])
            x = x + chunk_attention_mixer(h, k_pad, v_pad, b_w_qz[i],
                                          b_rel_bias[i], b_w_out[i])
    return rms_norm(x, final_norm_g)
```

```python
import numpy as np
from contextlib import ExitStack

import concourse.bass as bass
import concourse.mybir as mybir
from concourse.bass_utils import run_bass_kernel_spmd

F32 = mybir.dt.float32
BF16 = mybir.dt.bfloat16
AF = mybir.ActivationFunctionType
ALU = mybir.AluOpType

D = 1024
SEQ = 4096
NCORE = 8
SPLIT = 2304
HALO = 512
NT = SPLIT
TT = 256
NTT = NT // TT
NHALO_TT = 0
EPS = 1e-6
NDQ = 640
import os as _os
EXCL_SETUP_BARRIER = _os.environ.get('K_EXCL', '1') == '1'
FINE_W = _os.environ.get('K_FINEW', '0') == '1'
PREFETCH_B = _os.environ.get('K_PREB', '1') == '1'
POOL_SQ = _os.environ.get('K_POOLSQ', '1') == '1'
POOL_Y = _os.environ.get('K_POOLY', '1') == '1'


class TL:
    def __init__(self, sem, name):
        self.sem = sem
        self.count = 0
        self.name = name


class Buf:
    __slots__ = ("name", "w", "r", "stamp")

    def __init__(self, name):
        self.name = name
        self.w = None
        self.r = {}
        self.stamp = 0


class Eng:
    def __init__(self, eng, tl):
        self.eng = eng
        self.tl = tl
        self.seen = {}

    def wait(self, ev):
        tl, val = ev
        if self.seen.get(tl, 0) >= val:
            return
        self.eng.wait_ge(tl.sem, val)
        self.seen[tl] = val


class Ctx:
    def __init__(self, nc, es):
        self.nc = nc
        self.es = es
        self.tls = []
        self.PE = Eng(nc.tensor, self.new_tl("tl_pe"))
        self.ACT = Eng(nc.scalar, self.new_tl("tl_act"))
        self.DVE = Eng(nc.vector, self.new_tl("tl_dve"))
        self.POOL = Eng(nc.gpsimd, self.new_tl("tl_pool"))
        self.SP = Eng(nc.sync, self.new_tl("tl_sp"))
        self.engs = [self.PE, self.ACT, self.DVE, self.POOL, self.SP]
        self.banks = []
        self.bank_bufs = []
        self.pinned = set()
        self.clock = 0

    def new_tl(self, name):
        sem = self.es.enter_context(self.nc.semaphore(name))
        tl = TL(sem, name)
        self.tls.append(tl)
        return tl

    def _hazards(self, E, reads, writes, acc):
        for b in reads:
            if b.w is not None:
                E.wait(b.w)
        for b in writes:
            if b.w is not None and not (acc and b.w[0] is E.tl):
                E.wait(b.w)
            for tl, val in b.r.items():
                E.wait((tl, val))

    def _record(self, ev, reads, writes):
        tl, val = ev
        self.clock += 1
        for b in reads:
            b.stamp = self.clock
            if b.r.get(tl, 0) < val:
                b.r[tl] = val
        for b in writes:
            b.stamp = self.clock
            b.w = ev
            b.r = {}

    def op(self, E, fn, reads=(), writes=(), acc=False):
        self._hazards(E, reads, writes, acc)
        ins = fn()
        E.tl.count += 1
        ins.then_inc(E.tl.sem, 1)
        ev = (E.tl, E.tl.count)
        self._record(ev, reads, writes)
        return ev

    def dma(self, E, out, in_, tl, reads=(), writes=(), war=()):
        for b in reads:
            if b.w is not None:
                E.wait(b.w)
        for b in war:
            if b.w is not None:
                E.wait(b.w)
            for rtl, val in b.r.items():
                E.wait((rtl, val))
        for b in writes:
            if b.w is not None and b.w[0] is not tl:
                E.wait(b.w)
            for rtl, val in b.r.items():
                E.wait((rtl, val))
        ins = E.eng.dma_start(out=out, in_=in_)
        tl.count += 16
        ins.then_inc(tl.sem, 16)
        ev = (tl, tl.count)
        self._record(ev, reads, writes)
        return ev

    @staticmethod
    def seal(tl, bufs):
        for b in bufs:
            b.w = (tl, tl.count)

    def barrier(self, exclude=()):
        evs = [(tl, tl.count) for tl in self.tls if tl.count > 0 and tl not in exclude]
        for E in self.engs:
            for ev in evs:
                E.wait(ev)

    def next_bank(self, pin=False):
        cand = [i for i in range(8) if i not in self.pinned]
        if not cand:
            raise RuntimeError("no free psum bank")
        i = min(cand, key=lambda j: self.bank_bufs[j].stamp)
        self.clock += 1
        self.bank_bufs[i].stamp = self.clock
        if pin:
            self.pinned.add(i)
        return i

    def unpin(self, i):
        self.pinned.discard(i)


def build_program(mode="full"):
    nc = bass.Bass("TRN2", target_bir_lowering=False)

    def din(name, shape):
        return nc.dram_tensor(name, list(shape), F32, kind="ExternalInput").ap()

    x_t = din("x_t", [D, NT])
    g_a = din("g_a", [128, 8])
    w_in_d = din("w_in", [D, 6144])
    lng_d = din("lng", [128, 16])
    lnb_d = din("lnb", [128, 2048])
    wsT_d = din("wsT", [128, 16, 128])
    bs_d = din("bs", [1, 2048])
    w_oa_d = din("w_oa", [2048, D])
    g_kv = din("g_kv", [128, 8])
    w_kv_d = din("w_kv", [D, 2048])
    g_b = din("g_b", [128, 8])
    w_qz_d = din("w_qz", [D, 2048])
    biasT_d = din("biasT", [128, 16 * NDQ])
    maskT_d = din("maskT", [128, NDQ])
    w_ob_d = din("w_ob", [D, D])
    g_f = din("g_f", [128, 8])

    if mode == "A":
        out_d = nc.dram_tensor("x1_t", [D, NT], F32, kind="ExternalOutput").ap()
        x1s = out_d
    else:
        out_d = nc.dram_tensor("out_t", [D, NT], F32, kind="ExternalOutput").ap()
        x1s = nc.dram_tensor("x1_scratch", [D, NT], F32, kind="Internal").ap()

    x_v = x_t.rearrange("(k p) t -> p k t", p=128)
    x1_v = x1s.rearrange("(k p) t -> p k t", p=128)
    out_v = out_d.rearrange("(k p) t -> p k t", p=128)

    with ExitStack() as es:
        cx = Ctx(nc, es)
        PE, ACT, DVE, POOL, SP = cx.PE, cx.ACT, cx.DVE, cx.POOL, cx.SP

        for i in range(8):
            cx.banks.append(es.enter_context(nc.psum_tensor(f"bank{i}", [128, 512], F32)))
            cx.bank_bufs.append(Buf(f"bank{i}"))

        def newbank(pin=False):
            bi = cx.next_bank(pin)
            return bi, cx.banks[bi], cx.bank_bufs[bi]

        ones_bf = es.enter_context(nc.sbuf_tensor("ones_bf", [128, 128], BF16))
        b_ones = Buf("ones")
        cx.op(DVE, lambda: nc.vector.memset(ones_bf[:], 1.0), writes=[b_ones])
        eps_t = es.enter_context(nc.sbuf_tensor("eps_t", [128, 1], F32))
        b_eps = Buf("eps")
        cx.op(DVE, lambda: nc.vector.memset(eps_t[:], EPS), writes=[b_eps])
        mhalf_t = es.enter_context(nc.sbuf_tensor("mhalf_t", [128, 1], F32))
        b_mhalf = Buf("mhalf")
        cx.op(POOL, lambda: nc.gpsimd.memset(mhalf_t[:], -0.5), writes=[b_mhalf])

        x1_bufs = [Buf(f"x1s{t}") for t in range(NTT)]

        w_in = es.enter_context(nc.sbuf_tensor("w_in_sb", [128, 8, 6144], BF16))
        b_wk, b_wv, b_wq, b_wz, b_wob = Buf("wk"), Buf("wv"), Buf("wq"), Buf("wz"), Buf("wob")
        tl_wk, tl_wv, tl_wq, tl_wz, tl_wob = (cx.new_tl(n) for n in ("tl_bwk", "tl_bwv", "tl_bwq", "tl_bwz", "tl_bwob"))
        tlB_w = (tl_wk, tl_wv, tl_wq, tl_wz, tl_wob)
        w_kv = w_in[:, :, 2048:4096]
        w_qz = w_in[:, :, 0:2048]
        w_ob = w_in[:, :, 4096:5120]
        KT = w_in[:, :, 5120:6144].rearrange("p k (s t) -> p k s t", s=4)
        w_kv_v = w_kv_d.rearrange("(k p) n -> p k n", p=128)
        w_qz_v = w_qz_d.rearrange("(k p) n -> p k n", p=128)
        w_ob_v = w_ob_d.rearrange("(k p) n -> p k n", p=128)

        def rms_rstd(xT, xT_b, sq, sq_b, rs, rs_b, rstd, rstd_b, part=0, pool_sq=True):
            if part in (0, 1):
                for k in range(8):
                    if POOL_SQ and pool_sq:
                        cx.op(POOL, lambda k=k: nc.gpsimd.tensor_tensor(out=sq[:, k, :], in0=xT[:, k, :], in1=xT[:, k, :],
                                                                       op=ALU.mult),
                              reads=[xT_b[k]], writes=[sq_b[k]])
                    else:
                        cx.op(ACT, lambda k=k: nc.scalar.activation(out=sq[:, k, :], in_=xT[:, k, :], func=AF.Square),
                              reads=[xT_b[k]], writes=[sq_b[k]])
            if part in (0, 2):
                bi, bank, bb = newbank()

                def mm():
                    ins = None
                    for k in range(8):
                        ins = nc.tensor.matmul(bank[:, 0:TT], ones_bf[:], sq[:, k, :], start=(k == 0), stop=(k == 7))
                    return ins
                cx.op(PE, mm, reads=[b_ones] + sq_b, writes=[bb])
                cx.op(ACT, lambda: nc.scalar.activation(out=rs[:], in_=bank[:, 0:TT], func=AF.Ln,
                                                        bias=eps_t[:], scale=1.0 / D),
                      reads=[bb, b_eps], writes=[rs_b])
                cx.op(ACT, lambda: nc.scalar.activation(out=rstd[:], in_=rs[:], func=AF.Exp, scale=-0.5),
                      reads=[rs_b], writes=[rstd_b])

        with ExitStack() as esA:
            def sb(name, shape, dt):
                return esA.enter_context(nc.sbuf_tensor(name, list(shape), dt))

            w_oa = sb("w_oa_sb", [128, 16, 1024], BF16)
            wsT_bf = sb("wsT_bf", [128, 16, 128], BF16)
            biasH = sb("biasH", [128, 16, 128], F32)
            ga_t = sb("ga_t", [128, 8], F32)
            lng_t = sb("lng_t", [128, 16], F32)

            b_wvA, tl_wvA = Buf("w_v"), cx.new_tl("tl_wvA")
            b_wu = [Buf(f"w_u{g}") for g in range(4)]
            b_wzA = [Buf(f"w_z{g}") for g in range(4)]
            tl_wu = [cx.new_tl(f"tl_wu{g}") for g in range(4)]
            tl_wzA = [cx.new_tl(f"tl_wzA{g}") for g in range(4)]
            b_woa = [Buf(f"w_oa{j}") for j in range(4)]
            tl_woa = [cx.new_tl(f"tl_woa{j}") for j in range(4)]
            b_wsT, b_biasH, b_ga, b_lng = Buf("wsT"), Buf("biasH"), Buf("ga"), Buf("lng")
            tl_small = cx.new_tl("tl_smallA")

            cx.dma(SP, ga_t[:], g_a[:, :], tl_small, writes=[b_ga])
            cx.dma(SP, lng_t[:], lng_d[:, :], tl_small, writes=[b_lng])

            w_in_v = w_in_d.rearrange("(k p) n -> p k n", p=128)
            w_oa_v = w_oa_d.rearrange("(k p) n -> p k n", p=128)
            for k in range(8):
                cx.dma(POOL, w_in[:, k, 2048:4096], w_in_v[:, k, 2048:4096], tl_wvA, writes=[b_wvA])
            if FINE_W:
                for g in range(2):
                    for (c0, tls_, bs_) in ((1024 * g, tl_wu, b_wu), (4096 + 1024 * g, tl_wzA, b_wzA)):
                        for k in range(8):
                            cx.dma(POOL, w_in[:, k, c0:c0 + 1024], w_in_v[:, k, c0:c0 + 1024], tls_[2 * g],
                                   writes=bs_[2 * g:2 * g + 2])
            else:
                for (c0, tls_, bs_) in ((0, tl_wu, b_wu), (4096, tl_wzA, b_wzA)):
                    for k in range(8):
                        cx.dma(POOL, w_in[:, k, c0:c0 + 2048], w_in_v[:, k, c0:c0 + 2048], tls_[0], writes=bs_)

            with ExitStack() as esS:
                wsT_f = esS.enter_context(nc.sbuf_tensor("wsT_f", [128, 16, 128], F32))
                lnb_t = esS.enter_context(nc.sbuf_tensor("lnb_t", [128, 2048], F32))
                bs_t = esS.enter_context(nc.sbuf_tensor("bs_t", [1, 2048], F32))
                ones_f = esS.enter_context(nc.sbuf_tensor("ones_f", [1, 128], F32))
                b_wsf, b_lnb, b_bs, b_onesf = Buf("wsT_f"), Buf("lnb"), Buf("bs"), Buf("ones_f")
                cx.dma(SP, wsT_f[:], wsT_d[:, :, :], tl_small, writes=[b_wsf])
                cx.dma(SP, lnb_t[:], lnb_d[:, :], tl_small, writes=[b_lnb])
                cx.dma(SP, bs_t[:], bs_d[:, :], tl_small, writes=[b_bs])
                cx.seal(tl_small, [b_ga, b_lng, b_wsf, b_lnb, b_bs])
                cx.op(DVE, lambda: nc.vector.memset(ones_f[:], 1.0), writes=[b_onesf])
                cx.op(DVE, lambda: nc.vector.memset(wsT_f[64:128, :, 0:64], 0.0), writes=[b_wsf])
                cx.op(DVE, lambda: nc.vector.tensor_copy(out=wsT_bf[:], in_=wsT_f[:]), reads=[b_wsf], writes=[b_wsT])
                cx.op(DVE, lambda: nc.vector.tensor_scalar(out=lng_t[:], in0=lng_t[:], scalar1=0.5, scalar2=None,
                                                           op0=ALU.mult), reads=[b_lng], writes=[b_lng])
                for g in range(4):
                    bi, bank, bb = newbank()

                    def mm(g=g, bank=bank):
                        ins = None
                        for j in range(4):
                            h = g * 4 + j
                            nc.tensor.matmul(bank[:, j * 128:(j + 1) * 128], lnb_t[:, h * 128:(h + 1) * 128],
                                             wsT_f[:, h, :], start=True, stop=False)
                            ins = nc.tensor.matmul(bank[:, j * 128:(j + 1) * 128], ones_f[:, :],
                                                   bs_t[:, h * 128:(h + 1) * 128], start=False, stop=True)
                        return ins
                    cx.op(PE, mm, reads=[b_lnb, b_wsf, b_onesf, b_bs], writes=[bb])
                    cx.op(DVE, lambda g=g, bank=bank: nc.vector.tensor_scalar(
                        out=biasH[:, g * 4:(g + 1) * 4, :], in0=bank[:, :].rearrange("p (j t) -> p j t", j=4),
                        scalar1=0.5, scalar2=None, op0=ALU.mult),
                        reads=[bb], writes=[b_biasH])
                cx.barrier(exclude=([tl_wvA] + tl_wu + tl_wzA) if EXCL_SETUP_BARRIER else ())

            for j in range(4):
                for k in range(4 * j, 4 * j + 4):
                    cx.dma(POOL, w_oa[:, k, :], w_oa_v[:, k, :], tl_woa[j], writes=[b_woa[j]])

            xT = [sb(f"xT{i}", [128, 8, TT], F32) for i in range(2)]
            hT = [sb(f"hT{i}", [128, 8, TT], BF16) for i in range(2)]
            sq = sb("sq", [128, 8, TT], BF16)
            rs = sb("rs", [128, TT], F32)
            rstd = sb("rstd", [128, TT], F32)
            gv = sb("gv", [128, 2048], F32)
            nrm = sb("nrm", [128, 2048], BF16)
            stats = sb("stats", [128, 4, 6], F32)
            mv = sb("mv", [128, 2], F32)
            ve = sb("ve", [128, 1], F32)
            rsl = sb("rsl", [128, 1], F32)
            nb = sb("nb", [128, 1], F32)
            mixT = sb("mixT", [128, 16, TT], BF16)
            NGB = 3
            gu = [sb(f"gu{i}", [128, TT], F32) for i in range(NGB)]
            th = [sb(f"th{i}", [128, TT], F32) for i in range(NGB)]
            yT = sb("yT", [128, 16, TT], BF16)

            xT_b = [[Buf(f"xT{i}_{k}") for k in range(8)] for i in range(2)]
            hT_b = [[Buf(f"hT{i}_{k}") for k in range(8)] for i in range(2)]
            sq_b = [Buf(f"sq{k}") for k in range(8)]
            rs_b, rstd_b = Buf("rs"), Buf("rstd")
            gv_b = [Buf(f"gv{f}") for f in range(4)]
            nrm_b = Buf("nrm")
            stats_b = [Buf(f"stats{f}") for f in range(4)]
            mv_b, ve_b, rsl_b, nb_b = Buf("mv"), Buf("ve"), Buf("rsl"), Buf("nb")
            mix_b = [[Buf(f"mix{h}_{t}") for t in range(2)] for h in range(16)]
            gu_b = [Buf(f"gu{i}") for i in range(NGB)]
            th_b = [Buf(f"th{i}") for i in range(NGB)]
            yT_b = [Buf(f"yT{h}") for h in range(16)]
            tl_x = [cx.new_tl(f"tl_xload{i}") for i in range(2)]
            tl_x1st = [cx.new_tl(f"tl_x1store{i}") for i in range(2)]
            uz_bank = {}

            def A_load(ti):
                s = ti % 2
                cx.dma(SP, xT[s][:], x_v[:, :, ti * TT:(ti + 1) * TT], tl_x[s], writes=xT_b[s])

            def A_front(ti, part):
                s = ti % 2
                rms_rstd(xT[s], xT_b[s], sq, sq_b, rs, rs_b, rstd, rstd_b, part=part)
                if part in (0, 2):
                    for k in range(8):
                        cx.op(DVE, lambda k=k: nc.vector.scalar_tensor_tensor(
                            out=hT[s][:, k, :], in0=xT[s][:, k, :], scalar=ga_t[:, k:k + 1], in1=rstd[:],
                            op0=ALU.mult, op1=ALU.mult),
                            reads=[xT_b[s][k], b_ga, rstd_b], writes=[hT_b[s][k]])

            def A_v(ti, tt):
                s = ti % 2
                for fc in range(4):
                    bi, bank, bb = newbank()

                    def mm(fc=fc, bank=bank):
                        ins = None
                        for k in range(8):
                            ins = nc.tensor.matmul(bank[:, :], hT[s][:, k, tt * 128:(tt + 1) * 128],
                                                   w_in[:, k, 2048 + fc * 512:2048 + (fc + 1) * 512],
                                                   start=(k == 0), stop=(k == 7))
                        return ins
                    cx.op(PE, mm, reads=hT_b[s] + [b_wvA], writes=[bb])
                    cx.op(ACT, lambda fc=fc, bank=bank: nc.scalar.activation(
                        out=gv[:, fc * 512:(fc + 1) * 512], in_=bank[:, :], func=AF.Gelu_apprx_tanh),
                        reads=[bb], writes=[gv_b[fc]])
                    cx.op(DVE, lambda fc=fc: nc.vector.bn_stats(out=stats[:, fc, :], in_=gv[:, fc * 512:(fc + 1) * 512]),
                          reads=[gv_b[fc]], writes=[stats_b[fc]])
                cx.op(DVE, lambda: nc.vector.bn_aggr(out=mv[:], in_=stats[:, :, :].rearrange("p a b -> p (a b)")),
                      reads=stats_b, writes=[mv_b])
                cx.op(DVE, lambda: nc.vector.tensor_scalar(out=ve[:], in0=mv[:, 1:2], scalar1=EPS, scalar2=None, op0=ALU.add),
                      reads=[mv_b], writes=[ve_b])
                cx.op(POOL, lambda: nc.gpsimd.tensor_tensor(out=rsl[:], in0=ve[:], in1=mhalf_t[:], op=ALU.pow),
                      reads=[ve_b, b_mhalf], writes=[rsl_b])
                cx.op(DVE, lambda: nc.vector.tensor_scalar(out=nb[:], in0=mv[:, 0:1], scalar1=rsl[:, 0:1], scalar2=-1.0,
                                                           op0=ALU.mult, op1=ALU.mult),
                      reads=[mv_b, rsl_b], writes=[nb_b])

            def A_nrm(ti, tt):
                cx.op(ACT, lambda: nc.scalar.activation(out=nrm[:], in_=gv[:], func=AF.Identity,
                                                        bias=nb[:], scale=rsl[:]),
                      reads=gv_b + [nb_b, rsl_b], writes=[nrm_b])

            def A_spatial(ti, tt):
                for g in range(4):
                    bi, bank, bb = newbank()

                    def mm(g=g, bank=bank):
                        ins = None
                        for j in range(4):
                            h = g * 4 + j
                            ins = nc.tensor.matmul(bank[:, j * 128:(j + 1) * 128], nrm[:, h * 128:(h + 1) * 128],
                                                   wsT_bf[:, h, :], start=True, stop=True)
                        return ins
                    cx.op(PE, mm, reads=[nrm_b, b_wsT], writes=[bb])
                    for j in range(4):
                        h = g * 4 + j
                        cx.op(DVE, lambda h=h, j=j, bank=bank: nc.vector.scalar_tensor_tensor(
                            out=mixT[:, h, tt * 128:(tt + 1) * 128], in0=bank[:, j * 128:(j + 1) * 128],
                            scalar=lng_t[:, h:h + 1], in1=biasH[:, h, :], op0=ALU.mult, op1=ALU.add),
                            reads=[bb, b_lng, b_biasH], writes=[mix_b[h][tt]])

            def A_uz_mm(ti, h):
                s = ti % 2
                bi, bank, bb = newbank(pin=True)
                uz_bank[h] = (bank, bb, bi)

                def mm():
                    ins = None
                    for k in range(8):
                        nc.tensor.matmul(bank[:, 0:TT], w_in[:, k, h * 128:(h + 1) * 128], hT[s][:, k, :],
                                         start=(k == 0), stop=(k == 7))
                    for k in range(8):
                        ins = nc.tensor.matmul(bank[:, TT:2 * TT], w_in[:, k, 4096 + h * 128:4096 + (h + 1) * 128],
                                               hT[s][:, k, :], start=(k == 0), stop=(k == 7))
                    return ins
                cx.op(PE, mm, reads=hT_b[s] + [b_wu[h // 4], b_wzA[h // 4]], writes=[bb])
                i = h % NGB
                cx.op(ACT, lambda: nc.scalar.activation(out=gu[i][:], in_=bank[:, 0:TT], func=AF.Gelu_apprx_tanh),
                      reads=[bb], writes=[gu_b[i]])
                cx.op(ACT, lambda: nc.scalar.activation(out=th[i][:], in_=bank[:, TT:2 * TT], func=AF.Tanh, scale=0.5),
                      reads=[bb], writes=[th_b[i]])

            def A_uz_y(ti, h):
                bank, bb, bi = uz_bank[h]
                cx.unpin(bi)
                i = h % NGB
                cx.op(DVE, lambda: nc.vector.scalar_tensor_tensor(out=th[i][:], in0=th[i][:], scalar=1.0,
                                                                  in1=bank[:, TT:2 * TT], op0=ALU.add, op1=ALU.mult),
                      reads=[th_b[i], bb], writes=[th_b[i]])
                if POOL_Y:
                    cx.op(POOL, lambda: nc.gpsimd.tensor_tensor(out=gu[i][:], in0=gu[i][:], in1=mixT[:, h, :], op=ALU.mult),
                          reads=[gu_b[i]] + mix_b[h], writes=[gu_b[i]])
                    cx.op(DVE, lambda: nc.vector.tensor_tensor(out=yT[:, h, :], in0=th[i][:], in1=gu[i][:], op=ALU.mult),
                          reads=[th_b[i], gu_b[i]], writes=[yT_b[h]])
                else:
                    cx.op(DVE, lambda: nc.vector.tensor_tensor(out=th[i][:], in0=th[i][:], in1=gu[i][:], op=ALU.mult),
                          reads=[th_b[i], gu_b[i]], writes=[th_b[i]])
                    cx.op(DVE, lambda: nc.vector.tensor_tensor(out=yT[:, h, :], in0=th[i][:], in1=mixT[:, h, :], op=ALU.mult),
                          reads=[th_b[i]] + mix_b[h], writes=[yT_b[h]])

            def A_out(ti):
                s = ti % 2
                for db in range(8):
                    bi, bank, bb = newbank()

                    def mm(db=db, bank=bank):
                        ins = None
                        for kf in range(16):
                            ins = nc.tensor.matmul(bank[:, 0:TT], w_oa[:, kf, db * 128:(db + 1) * 128], yT[:, kf, :],
                                                   start=(kf == 0), stop=(kf == 15))
                        return ins
                    cx.op(PE, mm, reads=yT_b + b_woa, writes=[bb])
                    cx.op(DVE, lambda db=db, bank=bank: nc.vector.tensor_tensor(
                        out=xT[s][:, db, :], in0=bank[:, 0:TT], in1=xT[s][:, db, :], op=ALU.add),
                        reads=[bb, xT_b[s][db]], writes=[xT_b[s][db]])
                cx.dma(SP, x1_v[:, :, ti * TT:(ti + 1) * TT], xT[s][:], tl_x1st[s], reads=xT_b[s], writes=[x1_bufs[ti]])

            A_load(0)
            A_load(1)
            A_front(0, 0)
            A_v(0, 0)
            for ti in range(NTT):
                more = ti + 1 < NTT
                A_nrm(ti, 0)
                A_uz_mm(ti, 0)
                A_spatial(ti, 0)
                A_v(ti, 1)
                A_nrm(ti, 1)
                if not more and PREFETCH_B:
                    for (c0, tl, b) in ((0, tl_wk, b_wk), (1024, tl_wv, b_wv)):
                        for k in range(8):
                            cx.dma(POOL, w_kv[:, k, c0:c0 + 1024], w_kv_v[:, k, c0:c0 + 1024], tl, writes=[b], war=[b_wvA])
                A_uz_mm(ti, 1)
                A_uz_mm(ti, 2)
                A_spatial(ti, 1)
                A_uz_y(ti, 0)
                A_uz_y(ti, 1)
                A_uz_y(ti, 2)
                for h in range(3, 16):
                    A_uz_mm(ti, h)
                    A_uz_y(ti, h)
                    if more and h == 5:
                        A_front(ti + 1, 1)
                    if more and h == 9:
                        A_front(ti + 1, 2)
                if not more and PREFETCH_B:
                    for (c0, tl, b) in ((0, tl_wq, b_wq), (1024, tl_wz, b_wz)):
                        for k in range(8):
                            cx.dma(POOL, w_qz[:, k, c0:c0 + 1024], w_qz_v[:, k, c0:c0 + 1024], tl, writes=[b], war=b_wu)
                    for k in range(8):
                        cx.dma(POOL, w_ob[:, k, :], w_ob_v[:, k, :], tl_wob, writes=[b_wob], war=b_wzA)
                if more:
                    A_v(ti + 1, 0)
                A_out(ti)
                if ti + 2 < NTT:
                    A_load(ti + 2)

            cx.barrier(exclude=tlB_w)

        if mode == "A":
            return nc

        with ExitStack() as esB:
            def sb(name, shape, dt):
                return esB.enter_context(nc.sbuf_tensor(name, list(shape), dt))

            Mt = sb("Mt", [128, 16, NDQ], BF16)
            gkv_t = sb("gkv_t", [128, 8], F32)
            gb_t = sb("gb_t", [128, 8], F32)
            gf_t = sb("gf_t", [128, 8], F32)

            b_M = [Buf(f"M{h}") for h in range(16)]
            b_mask, b_gkv, b_gb, b_gf = Buf("mask"), Buf("gkv"), Buf("gb"), Buf("gf")
            tl_smallB = cx.new_tl("tl_smallB")

            if not PREFETCH_B:
                for (c0, tl, b) in ((0, tl_wk, b_wk), (1024, tl_wv, b_wv)):
                    for k in range(8):
                        cx.dma(POOL, w_kv[:, k, c0:c0 + 1024], w_kv_v[:, k, c0:c0 + 1024], tl, writes=[b])
                for (c0, tl, b) in ((0, tl_wq, b_wq), (1024, tl_wz, b_wz)):
                    for k in range(8):
                        cx.dma(POOL, w_qz[:, k, c0:c0 + 1024], w_qz_v[:, k, c0:c0 + 1024], tl, writes=[b])
                for k in range(8):
                    cx.dma(POOL, w_ob[:, k, :], w_ob_v[:, k, :], tl_wob, writes=[b_wob])
            mask_t = sb("mask_t", [128, NDQ], F32)
            NMS = 2
            Mst = [sb(f"Mst{i}", [128, NDQ], F32) for i in range(NMS)]
            Mst_b = [Buf(f"Mst{i}") for i in range(NMS)]
            tl_Mst = [cx.new_tl(f"tl_Mst{i}") for i in range(NMS)]
            cx.dma(SP, gkv_t[:], g_kv[:, :], tl_smallB, writes=[b_gkv])
            cx.dma(SP, gb_t[:], g_b[:, :], tl_smallB, writes=[b_gb])
            cx.dma(SP, gf_t[:], g_f[:, :], tl_smallB, writes=[b_gf])
            cx.dma(SP, mask_t[:], maskT_d[:, :], tl_smallB, writes=[b_mask])
            cx.seal(tl_smallB, [b_gkv, b_gb, b_gf, b_mask])

            def B_setup_M(h0, h1):
                for h in range(h0, h1):
                    i = h % NMS
                    cx.dma(SP, Mst[i][:], biasT_d[:, h * NDQ:(h + 1) * NDQ], tl_Mst[i], writes=[Mst_b[i]])
                    cx.op(ACT, lambda h=h, i=i: nc.scalar.activation(out=Mst[i][:], in_=Mst[i][:], func=AF.Exp),
                          reads=[Mst_b[i]], writes=[Mst_b[i]])
                    cx.op(DVE, lambda h=h, i=i: nc.vector.tensor_tensor(out=Mt[:, h, :], in0=Mst[i][:], in1=mask_t[:],
                                                                       op=ALU.mult),
                          reads=[Mst_b[i], b_mask], writes=[b_M[h]])

            xTs = [sb(f"xTb{i}", [128, 8, TT], F32) for i in range(2)]
            sq = sb("sqb", [128, 8, TT], BF16)
            rs = sb("rsb", [128, TT], F32)
            rstd = sb("rstdb", [128, TT], F32)
            xk = sb("xk", [128, 8, TT], BF16)
            xb = sb("xb", [128, 8, TT], BF16)
            Vr = sb("Vr", [128, 4, 2, 8, 192], BF16)
            QTp = sb("QTp", [128, 8, 2, TT], BF16)
            szT = sb("szT", [128, 8, TT], BF16)
            NSB = 1
            stmp = [sb(f"stmp{i}", [128, 2 * TT], F32) for i in range(NSB)]
            NEB = 4
            Eb = [sb(f"Eb{i}", [128, 2 * TT], BF16) for i in range(NEB)]
            Pb = [sb(f"Pb{i}", [128, 2 * TT], BF16) for i in range(NEB)]
            for i in range(NMS):
                Eb.append(Mst[i][:, 0:TT].bitcast(BF16))
                Pb.append(Mst[i][:, TT:2 * TT].bitcast(BF16))
            NEB_ALL = NEB + NMS
            rd = sb("rd", [128, TT], F32)
            rd2 = sb("rd2", [128, TT], F32)
            ogT = sb("ogT", [128, 8, TT], BF16)

            xTs_b = [[Buf(f"bxT{i}_{k}") for k in range(8)] for i in range(2)]
            sq_b = [Buf(f"bsq{k}") for k in range(8)]
            rs_b, rstd_b = Buf("brs"), Buf("brstd")
            xk_b = [Buf(f"xk{k}") for k in range(8)]
            xb_b = [Buf(f"xb{k}") for k in range(8)]
            KT_b = [[Buf(f"KT{s}_{p}") for p in range(8)] for s in range(4)]
            V_b = [[[Buf(f"V{s}_{t}_{f}") for f in range(2)] for t in range(2)] for s in range(4)]
            QT_b = [Buf(f"QT{p}") for p in range(8)]
            szT_b = [Buf(f"szT{p}") for p in range(8)]
            stmp_b = [Buf(f"stmp{i}") for i in range(NSB)]
            E_b = [Buf(f"E{i}") for i in range(NEB + NMS)]
            P_b = [Buf(f"P{i}") for i in range(NEB + NMS)]
            rd_b, rd2_b = Buf("rd"), Buf("rd2")
            og_b = [Buf(f"og{p}") for p in range(8)]
            tl_xb = [cx.new_tl(f"tl_x1load{i}") for i in range(2)]
            tl_out = [cx.new_tl(f"tl_outstore{i}") for i in range(2)]
            ecount = [0]

            for s in range(4):
                for t in range(2):
                    cx.op(DVE, lambda s=s, t=t: nc.vector.memset(Vr[:, s, t, :, 64:128], 1.0),
                          writes=[V_b[s][t][0], V_b[s][t][1]])
            cx.op(DVE, lambda: nc.vector.memset(QTp[64:128, :, 0, :], 0.0), writes=QT_b)
            cx.op(DVE, lambda: nc.vector.memset(QTp[0:64, :, 1, :], 0.0), writes=QT_b)

            def B_load(tj):
                sj = tj % 2
                cx.dma(SP, xTs[sj][:], x1_v[:, :, tj * TT:(tj + 1) * TT], tl_xb[sj], reads=[x1_bufs[tj]], writes=xTs_b[sj])

            def B_front(tj, pool_sq=True, part=0):
                sj = tj % 2
                xTj, xTj_b = xTs[sj], xTs_b[sj]
                rms_rstd(xTj, xTj_b, sq, sq_b, rs, rs_b, rstd, rstd_b, pool_sq=pool_sq, part=part)
                if part == 1:
                    return
                for k in range(8):
                    cx.op(DVE, lambda k=k: nc.vector.scalar_tensor_tensor(
                        out=xk[:, k, :], in0=xTj[:, k, :], scalar=gkv_t[:, k:k + 1], in1=rstd[:],
                        op0=ALU.mult, op1=ALU.mult),
                        reads=[xTj_b[k], b_gkv, rstd_b], writes=[xk_b[k]])
                if tj >= NHALO_TT:
                    for k in range(8):
                        cx.op(DVE, lambda k=k: nc.vector.scalar_tensor_tensor(
                            out=xb[:, k, :], in0=xTj[:, k, :], scalar=gb_t[:, k:k + 1], in1=rstd[:],
                            op0=ALU.mult, op1=ALU.mult),
                            reads=[xTj_b[k], b_gb, rstd_b], writes=[xb_b[k]])

            def B_K(ti, pp):
                slot = ti % 4
                bi, bank, bb = newbank()

                def mm():
                    ins = None
                    for j in range(2):
                        p = pp * 2 + j
                        for k in range(8):
                            ins = nc.tensor.matmul(bank[:, j * TT:(j + 1) * TT], w_kv[:, k, p * 128:(p + 1) * 128],
                                                   xk[:, k, :], start=(k == 0), stop=(k == 7))
                    return ins
                cx.op(PE, mm, reads=xk_b + [b_wk], writes=[bb])
                cx.op(ACT, lambda: nc.scalar.copy(
                    out=KT[:, 2 * pp:2 * pp + 2, slot, :], in_=bank[:, :].rearrange("p (j t) -> p j t", j=2)),
                    reads=[bb], writes=[KT_b[slot][2 * pp], KT_b[slot][2 * pp + 1]])

            def B_V(ti, tt, fc):
                slot = ti % 4
                bi, bank, bb = newbank()

                def mm():
                    ins = None
                    for k in range(8):
                        ins = nc.tensor.matmul(bank[:, :], xk[:, k, tt * 128:(tt + 1) * 128],
                                               w_kv[:, k, 1024 + fc * 512:1024 + (fc + 1) * 512],
                                               start=(k == 0), stop=(k == 7))
                    return ins
                cx.op(PE, mm, reads=xk_b + [b_wv], writes=[bb])
                bv = bank[:, :].rearrange("p (q c d) -> p q c d", q=4, c=2)
                for c in range(2):
                    cx.op(DVE, lambda c=c: nc.vector.tensor_copy(
                        out=Vr[:, slot, tt, 4 * fc:4 * fc + 4, 128 * c:128 * c + 64], in_=bv[:, :, c, :]),
                        reads=[bb], writes=[V_b[slot][tt][fc]])

            def B_KV(ti):
                for pp in range(4):
                    B_K(ti, pp)
                for tt in range(2):
                    for fc in range(2):
                        B_V(ti, tt, fc)

            def B_QZc(ti, pp):
                bq_i, bankq, bbq = newbank()

                def mmq():
                    ins = None
                    for j in range(2):
                        p = 2 * pp + j
                        for k in range(8):
                            ins = nc.tensor.matmul(bankq[:, j * TT:(j + 1) * TT], w_qz[:, k, p * 128:(p + 1) * 128],
                                                   xb[:, k, :], start=(k == 0), stop=(k == 7))
                    return ins
                cx.op(PE, mmq, reads=xb_b + [b_wq], writes=[bbq])
                qb = [QT_b[2 * pp], QT_b[2 * pp + 1]]
                cx.op(DVE, lambda: nc.vector.tensor_scalar(
                    out=QTp[0:64, 2 * pp:2 * pp + 2, 0, :], in0=bankq[0:64, :].rearrange("p (j t) -> p j t", j=2),
                    scalar1=0.125, scalar2=None, op0=ALU.mult),
                    reads=[bbq], writes=qb)
                cx.op(DVE, lambda: nc.vector.tensor_scalar(
                    out=QTp[64:128, 2 * pp:2 * pp + 2, 1, :], in0=bankq[64:128, :].rearrange("p (j t) -> p j t", j=2),
                    scalar1=0.125, scalar2=None, op0=ALU.mult),
                    reads=[bbq] + qb, writes=qb)
                bz_i, bankz, bbz = newbank()

                def mmz():
                    ins = None
                    for j in range(2):
                        p = 2 * pp + j
                        for k in range(8):
                            ins = nc.tensor.matmul(bankz[:, j * TT:(j + 1) * TT],
                                                   w_qz[:, k, 1024 + p * 128:1024 + (p + 1) * 128],
                                                   xb[:, k, :], start=(k == 0), stop=(k == 7))
                    return ins
                cx.op(PE, mmz, reads=xb_b + [b_wz], writes=[bbz])
                si = pp % NSB
                cx.op(ACT, lambda: nc.scalar.activation(out=stmp[si][:], in_=bankz[:, :], func=AF.Exp, scale=-1.0),
                      reads=[bbz], writes=[stmp_b[si]])
                cx.op(ACT, lambda: nc.scalar.activation(out=stmp[si][:], in_=stmp[si][:], func=AF.Ln, bias=1.0),
                      reads=[stmp_b[si]], writes=[stmp_b[si]])
                cx.op(ACT, lambda: nc.scalar.activation(out=stmp[si][:], in_=stmp[si][:], func=AF.Exp, scale=-1.0),
                      reads=[stmp_b[si]], writes=[stmp_b[si]])
                cx.op(DVE, lambda: nc.vector.tensor_tensor(
                    out=szT[:, 2 * pp:2 * pp + 2, :], in0=bankz[:, :].rearrange("p (j t) -> p j t", j=2),
                    in1=stmp[si][:].rearrange("p (j t) -> p j t", j=2), op=ALU.mult),
                    reads=[bbz, stmp_b[si]], writes=[szT_b[2 * pp], szT_b[2 * pp + 1]])

            def alias_stage_buffers():
                for i in range(NMS):
                    for b in (E_b[NEB + i], P_b[NEB + i]):
                        b.w = Mst_b[i].w
                        b.r = dict(Mst_b[i].r)

            def B_attn(ti):
                order = [i for i in (2, 3, 4, 1, 0, 5) if ti - 2 + i // 2 >= 0]
                steps = [(p, i) for p in range(8) for i in order]
                state = {}

                def geom(i):
                    q0, q1 = (0, 128) if i == 0 else ((128, 256) if i == 5 else (0, 256))
                    dq0 = 512 - 128 * i + q0
                    tsrc = ti - 2 + i // 2
                    return q0, q1, dq0, tsrc % 4, i % 2, False

                def emit_S(p, i):
                    q0, q1, dq0, ks, half, is_halo = geom(i)
                    n = q1 - q0
                    bi, bank, bb = newbank()
                    cx.op(PE, lambda: nc.tensor.matmul(
                        bank[:, 0:2 * n].rearrange("p (c n) -> p c n", c=2),
                        KT[:, p, ks, half * 128:(half + 1) * 128], QTp[:, p, :, q0:q1], start=True, stop=True),
                        reads=[KT_b[ks][p], QT_b[p]], writes=[bb])
                    e = ecount[0] % NEB_ALL
                    ecount[0] += 1
                    cx.op(ACT, lambda: nc.scalar.activation(out=Eb[e][:, 0:2 * n], in_=bank[:, 0:2 * n], func=AF.Exp),
                          reads=[bb], writes=[E_b[e]])
                    ev = Eb[e][:, 0:2 * n].rearrange("p (c n) -> p c n", c=2)
                    pv = Pb[e][:, 0:2 * n].rearrange("p (c n) -> p c n", c=2)
                    mv_ = Mt[:, 2 * p:2 * p + 2, dq0:dq0 + n]
                    cx.op(DVE, lambda: nc.vector.tensor_tensor(out=pv, in0=ev, in1=mv_, op=ALU.mult),
                          reads=[E_b[e], b_M[2 * p], b_M[2 * p + 1]], writes=[P_b[e]])
                    return e

                def emit_PV(p, i, e, first):
                    q0, q1, dq0, ks, half, is_halo = geom(i)
                    n = q1 - q0
                    bi = state["obank"]
                    bank, bb = cx.banks[bi], cx.bank_bufs[bi]

                    def mm():
                        nc.tensor.matmul(bank[:, q0:q1], Vr[:, ks, half, p, 0:128], Pb[e][:, 0:n],
                                         start=first, stop=False)
                        return nc.tensor.matmul(bank[:, TT + q0:TT + q1], Vr[:, ks, half, p, 64:192], Pb[e][:, n:2 * n],
                                                start=False, stop=(i == 5))
                    cx.op(PE, mm, reads=[V_b[ks][half][p // 4], P_b[e]], writes=[bb], acc=not first)

                def emit_fin1(p, bi):
                    bank, bb = cx.banks[bi], cx.bank_bufs[bi]
                    cx.op(ACT, lambda: nc.scalar.activation(out=rd[0:64, :], in_=bank[64:128, 0:TT], func=AF.Ln),
                          reads=[bb], writes=[rd_b])
                    cx.op(ACT, lambda: nc.scalar.activation(out=rd[64:128, :], in_=bank[0:64, TT:2 * TT], func=AF.Ln),
                          reads=[bb, rd_b], writes=[rd_b])
                    cx.op(ACT, lambda: nc.scalar.activation(out=rd[:], in_=rd[:], func=AF.Exp, scale=-1.0),
                          reads=[rd_b], writes=[rd_b])
                    return (p, bi)

                def emit_fin2(p, bi):
                    bank, bb = cx.banks[bi], cx.bank_bufs[bi]
                    cx.op(DVE, lambda: nc.vector.tensor_tensor(out=rd2[:], in0=rd[:], in1=szT[:, p, :], op=ALU.mult),
                          reads=[rd_b, szT_b[p]], writes=[rd2_b])
                    cx.op(DVE, lambda: nc.vector.tensor_tensor(out=ogT[0:64, p, :], in0=bank[0:64, 0:TT], in1=rd2[0:64, :],
                                                               op=ALU.mult),
                          reads=[bb, rd2_b], writes=[og_b[p]])
                    cx.op(DVE, lambda: nc.vector.tensor_tensor(out=ogT[64:128, p, :], in0=bank[64:128, TT:2 * TT],
                                                               in1=rd2[64:128, :], op=ALU.mult),
                          reads=[bb, rd2_b, og_b[p]], writes=[og_b[p]])
                    cx.unpin(bi)

                fin1_q, fin2_q = [], []
                LOOK = 5
                sched = {}
                nst = len(steps)

                def at(s_, fn_):
                    sched.setdefault((s_ * nst) // 48, []).append(fn_)
                for pp_, s_ in ((1, 2), (2, 8), (3, 13)):
                    f_ = steps.index((2 * pp_, order[0]))
                    if f_ - LOOK - 1 < 0:
                        B_QZc(ti, pp_)
                    else:
                        assert (s_ * nst) // 48 <= f_ - LOOK - 1, (ti, pp_)
                        at(s_, lambda pp_=pp_: B_QZc(ti, pp_))
                if ti - 1 >= 0:
                    at(1, lambda: B_final1(ti - 1, part=1))
                    at(8, lambda: B_final1(ti - 1, part=2))
                    at(10, lambda: B_final2(ti - 1))
                    if ti + 1 < NTT:
                        at(10, lambda: B_load(ti + 1))
                if ti + 1 < NTT:
                    at(15, lambda: B_front(ti + 1, part=1))
                    at(19, lambda: B_front(ti + 1, part=2))
                    for j_, s_ in enumerate((21, 24, 27, 30)):
                        at(s_, lambda j_=j_: B_K(ti + 1, j_))
                    for j_, s_ in enumerate((33, 36, 39, 42)):
                        at(s_, lambda j_=j_: B_V(ti + 1, j_ // 2, j_ % 2))
                pend = [emit_S(*steps[j]) for j in range(LOOK)]
                for s, (p, i) in enumerate(steps):
                    for fn_ in sched.pop(s, ()):
                        fn_()
                    if s + LOOK < len(steps):
                        pend.append(emit_S(*steps[s + LOOK]))
                    while fin2_q and fin2_q[0][0] <= s:
                        _, a_ = fin2_q.pop(0)
                        emit_fin2(*a_)
                    while fin1_q and fin1_q[0][0] <= s:
                        if fin2_q:
                            _, a_ = fin2_q.pop(0)
                            emit_fin2(*a_)
                        _, pp_ = fin1_q.pop(0)
                        fin2_q.append((s + 2, emit_fin1(*pp_)))
                    first = (i == order[0])
                    if first:
                        state["obank"] = cx.next_bank(pin=True)
                    emit_PV(p, i, pend.pop(0), first)
                    if i == 5:
                        fin1_q.append((s + 2, (p, state["obank"])))
                for s_ in sorted(sched):
                    for fn_ in sched[s_]:
                        fn_()
                for _, a_ in fin2_q:
                    emit_fin2(*a_)
                for _, pp_ in fin1_q:
                    emit_fin2(*emit_fin1(*pp_))


            def B_out(ti):
                xT, xT_b = xTs[ti % 2], xTs_b[ti % 2]
                for db in range(8):
                    bi, bank, bb = newbank()

                    def mm(db=db, bank=bank):
                        ins = None
                        for kp in range(8):
                            ins = nc.tensor.matmul(bank[:, 0:TT], w_ob[:, kp, db * 128:(db + 1) * 128], ogT[:, kp, :],
                                                   start=(kp == 0), stop=(kp == 7))
                        return ins
                    cx.op(PE, mm, reads=og_b + [b_wob], writes=[bb])
                    cx.op(DVE, lambda db=db, bank=bank: nc.vector.tensor_tensor(
                        out=xT[:, db, :], in0=bank[:, 0:TT], in1=xT[:, db, :], op=ALU.add),
                        reads=[bb, xT_b[db]], writes=[xT_b[db]])

            def B_final1(ti, part=0):
                xT, xT_b = xTs[ti % 2], xTs_b[ti % 2]
                rms_rstd(xT, xT_b, sq, sq_b, rs, rs_b, rstd, rstd_b, part=part)

            def B_final2(ti):
                t0 = ti * TT
                xT, xT_b = xTs[ti % 2], xTs_b[ti % 2]
                for k in range(8):
                    cx.op(DVE, lambda k=k: nc.vector.scalar_tensor_tensor(
                        out=xT[:, k, :], in0=xT[:, k, :], scalar=gf_t[:, k:k + 1], in1=rstd[:],
                        op0=ALU.mult, op1=ALU.mult),
                        reads=[xT_b[k], b_gf, rstd_b], writes=[xT_b[k]])
                o0 = t0
                cx.dma(SP, out_v[:, :, o0:o0 + TT], xT[:], tl_out[ti % 2], reads=xT_b)

            B_load(0)
            B_load(1)
            B_front(0, pool_sq=False)
            B_setup_M(0, 2)
            B_KV(0)
            B_setup_M(2, 8)
            B_QZc(0, 0)
            B_setup_M(8, 16)
            alias_stage_buffers()
            for ti in range(NTT):
                B_attn(ti)
                if ti + 1 < NTT:
                    B_QZc(ti + 1, 0)
                B_out(ti)
            B_final1(NTT - 1)
            B_final2(NTT - 1)

            cx.barrier()
            for t in tl_out:
                SP.wait((t, t.count))
    return nc


def _chunk_cols(v):
    return np.ascontiguousarray(np.asarray(v, dtype=np.float32).reshape(-1, 128).T)


def make_in_maps(x, a_norm_g, a_w_in, a_ln_g, a_ln_b, a_w_s, a_b_s, a_w_out, kv_norm_g, w_kv, b_norm_g,
                 b_w_qz, b_rel_bias, b_w_out, final_norm_g):
    x = np.asarray(x, dtype=np.float32)
    ki = np.arange(128)[:, None]
    dq = np.arange(NDQ)[None, :]
    idx = np.clip(dq - ki, -256, 256) + 256
    rb = np.asarray(b_rel_bias, dtype=np.float32)[0]
    biasT = np.ascontiguousarray(np.transpose(rb[:, idx], (1, 0, 2))).reshape(128, 16 * NDQ)
    cd = dq // 64 - ki // 64
    maskT = ((cd >= 0) & (cd <= 8)).astype(np.float32)
    shared = {
        "g_a": _chunk_cols(a_norm_g[0]),
        "w_in": np.ascontiguousarray(a_w_in[0], dtype=np.float32),
        "lng": _chunk_cols(a_ln_g[0]),
        "lnb": np.ascontiguousarray(np.broadcast_to(np.asarray(a_ln_b[0], dtype=np.float32)[None, :], (128, 2048))),
        "wsT": np.ascontiguousarray(np.transpose(np.asarray(a_w_s[0], dtype=np.float32), (2, 0, 1))),
        "bs": np.ascontiguousarray(np.asarray(a_b_s[0], dtype=np.float32).reshape(1, 2048)),
        "w_oa": np.ascontiguousarray(a_w_out[0], dtype=np.float32),
        "g_kv": _chunk_cols(kv_norm_g),
        "w_kv": np.ascontiguousarray(w_kv, dtype=np.float32),
        "g_b": _chunk_cols(b_norm_g[0]),
        "w_qz": np.ascontiguousarray(b_w_qz[0], dtype=np.float32),
        "biasT": biasT,
        "maskT": maskT,
        "w_ob": np.ascontiguousarray(b_w_out[0], dtype=np.float32),
        "g_f": _chunk_cols(final_norm_g),
    }
    in_maps = []
    for c in range(NCORE):
        b, half = c // 2, c % 2
        lo = 0 if half == 0 else SPLIT - HALO
        m = dict(shared)
        m["x_t"] = np.ascontiguousarray(x[b, lo:lo + NT, :].T)
        in_maps.append(m)
    return in_maps


_NC_CACHE = {}


def kernel(**inputs):
    in_maps = make_in_maps(**inputs)
    if "full" not in _NC_CACHE:
        _NC_CACHE["full"] = build_program("full")
    nc = _NC_CACHE["full"]
    res = run_bass_kernel_spmd(nc, in_maps, core_ids=list(range(NCORE)))
    out = np.empty((4, SEQ, D), dtype=np.float32)
    for c in range(NCORE):
        b, half = c // 2, c % 2
        o = res.results[c]["out_t"]
        if half == 0:
            out[b, 0:SPLIT, :] = o.T
        else:
            out[b, SPLIT:SEQ, :] = o[:, HALO:].T
    return out
```

```python
import numpy as np
from contextlib import ExitStack

import concourse.bass as bass
import concourse.mybir as mybir
from concourse.bass_utils import run_bass_kernel_spmd

F32 = mybir.dt.float32
BF16 = mybir.dt.bfloat16
AF = mybir.ActivationFunctionType
ALU = mybir.AluOpType

D = 1024
SEQ = 4096
NCORE = 8
SPLIT = 2304
HALO = 512
NT = SPLIT
TT = 256
NTT = NT // TT
NHALO_TT = 0
EPS = 1e-6
NDQ = 640
import os as _os
EXCL_SETUP_BARRIER = _os.environ.get('K_EXCL', '1') == '1'
FINE_W = _os.environ.get('K_FINEW', '0') == '1'
PREFETCH_B = _os.environ.get('K_PREB', '1') == '1'
POOL_SQ = _os.environ.get('K_POOLSQ', '1') == '1'
POOL_Y = _os.environ.get('K_POOLY', '1') == '1'


class TL:
    def __init__(self, sem, name):
        self.sem = sem
        self.count = 0
        self.name = name


class Buf:
    __slots__ = ("name", "w", "r", "stamp")

    def __init__(self, name):
        self.name = name
        self.w = None
        self.r = {}
        self.stamp = 0


class Eng:
    def __init__(self, eng, tl):
        self.eng = eng
        self.tl = tl
        self.seen = {}

    def wait(self, ev):
        tl, val = ev
        if self.seen.get(tl, 0) >= val:
            return
        self.eng.wait_ge(tl.sem, val)
        self.seen[tl] = val


class Ctx:
    def __init__(self, nc, es):
        self.nc = nc
        self.es = es
        self.tls = []
        self.PE = Eng(nc.tensor, self.new_tl("tl_pe"))
        self.ACT = Eng(nc.scalar, self.new_tl("tl_act"))
        self.DVE = Eng(nc.vector, self.new_tl("tl_dve"))
        self.POOL = Eng(nc.gpsimd, self.new_tl("tl_pool"))
        self.SP = Eng(nc.sync, self.new_tl("tl_sp"))
        self.engs = [self.PE, self.ACT, self.DVE, self.POOL, self.SP]
        self.banks = []
        self.bank_bufs = []
        self.pinned = set()
        self.clock = 0

    def new_tl(self, name):
        sem = self.es.enter_context(self.nc.semaphore(name))
        tl = TL(sem, name)
        self.tls.append(tl)
        return tl

    def _hazards(self, E, reads, writes, acc):
        for b in reads:
            if b.w is not None:
                E.wait(b.w)
        for b in writes:
            if b.w is not None and not (acc and b.w[0] is E.tl):
                E.wait(b.w)
            for tl, val in b.r.items():
                E.wait((tl, val))

    def _record(self, ev, reads, writes):
        tl, val = ev
        self.clock += 1
        for b in reads:
            b.stamp = self.clock
            if b.r.get(tl, 0) < val:
                b.r[tl] = val
        for b in writes:
            b.stamp = self.clock
            b.w = ev
            b.r = {}

    def op(self, E, fn, reads=(), writes=(), acc=False):
        self._hazards(E, reads, writes, acc)
        ins = fn()
        E.tl.count += 1
        ins.then_inc(E.tl.sem, 1)
        ev = (E.tl, E.tl.count)
        self._record(ev, reads, writes)
        return ev

    def dma(self, E, out, in_, tl, reads=(), writes=(), war=()):
        for b in reads:
            if b.w is not None:
                E.wait(b.w)
        for b in war:
            if b.w is not None:
                E.wait(b.w)
            for rtl, val in b.r.items():
                E.wait((rtl, val))
        for b in writes:
            if b.w is not None and b.w[0] is not tl:
                E.wait(b.w)
            for rtl, val in b.r.items():
                E.wait((rtl, val))
        ins = E.eng.dma_start(out=out, in_=in_)
        tl.count += 16
        ins.then_inc(tl.sem, 16)
        ev = (tl, tl.count)
        self._record(ev, reads, writes)
        return ev

    @staticmethod
    def seal(tl, bufs):
        for b in bufs:
            b.w = (tl, tl.count)

    def barrier(self, exclude=()):
        evs = [(tl, tl.count) for tl in self.tls if tl.count > 0 and tl not in exclude]
        for E in self.engs:
            for ev in evs:
                E.wait(ev)

    def next_bank(self, pin=False):
        cand = [i for i in range(8) if i not in self.pinned]
        if not cand:
            raise RuntimeError("no free psum bank")
        i = min(cand, key=lambda j: self.bank_bufs[j].stamp)
        self.clock += 1
        self.bank_bufs[i].stamp = self.clock
        if pin:
            self.pinned.add(i)
        return i

    def unpin(self, i):
        self.pinned.discard(i)


def build_program(mode="full"):
    nc = bass.Bass("TRN2", target_bir_lowering=False)

    def din(name, shape):
        return nc.dram_tensor(name, list(shape), F32, kind="ExternalInput").ap()

    x_t = din("x_t", [D, NT])
    g_a = din("g_a", [128, 8])
    w_in_d = din("w_in", [D, 6144])
    lng_d = din("lng", [128, 16])
    lnb_d = din("lnb", [128, 2048])
    wsT_d = din("wsT", [128, 16, 128])
    bs_d = din("bs", [1, 2048])
    w_oa_d = din("w_oa", [2048, D])
    g_kv = din("g_kv", [128, 8])
    w_kv_d = din("w_kv", [D, 2048])
    g_b = din("g_b", [128, 8])
    w_qz_d = din("w_qz", [D, 2048])
    biasT_d = din("biasT", [128, 16 * NDQ])
    maskT_d = din("maskT", [128, NDQ])
    w_ob_d = din("w_ob", [D, D])
    g_f = din("g_f", [128, 8])

    if mode == "A":
        out_d = nc.dram_tensor("x1_t", [D, NT], F32, kind="ExternalOutput").ap()
        x1s = out_d
    else:
        out_d = nc.dram_tensor("out_t", [D, NT], F32, kind="ExternalOutput").ap()
        x1s = nc.dram_tensor("x1_scratch", [D, NT], F32, kind="Internal").ap()

    x_v = x_t.rearrange("(k p) t -> p k t", p=128)
    x1_v = x1s.rearrange("(k p) t -> p k t", p=128)
    out_v = out_d.rearrange("(k p) t -> p k t", p=128)

    with ExitStack() as es:
        cx = Ctx(nc, es)
        PE, ACT, DVE, POOL, SP = cx.PE, cx.ACT, cx.DVE, cx.POOL, cx.SP

        for i in range(8):
            cx.banks.append(es.enter_context(nc.psum_tensor(f"bank{i}", [128, 512], F32)))
            cx.bank_bufs.append(Buf(f"bank{i}"))

        def newbank(pin=False):
            bi = cx.next_bank(pin)
            return bi, cx.banks[bi], cx.bank_bufs[bi]

        ones_bf = es.enter_context(nc.sbuf_tensor("ones_bf", [128, 128], BF16))
        b_ones = Buf("ones")
        cx.op(DVE, lambda: nc.vector.memset(ones_bf[:], 1.0), writes=[b_ones])
        eps_t = es.enter_context(nc.sbuf_tensor("eps_t", [128, 1], F32))
        b_eps = Buf("eps")
        cx.op(DVE, lambda: nc.vector.memset(eps_t[:], EPS), writes=[b_eps])
        mhalf_t = es.enter_context(nc.sbuf_tensor("mhalf_t", [128, 1], F32))
        b_mhalf = Buf("mhalf")
        cx.op(POOL, lambda: nc.gpsimd.memset(mhalf_t[:], -0.5), writes=[b_mhalf])

        x1_bufs = [Buf(f"x1s{t}") for t in range(NTT)]

        w_in = es.enter_context(nc.sbuf_tensor("w_in_sb", [128, 8, 6144], BF16))
        b_wk, b_wv, b_wq, b_wz, b_wob = Buf("wk"), Buf("wv"), Buf("wq"), Buf("wz"), Buf("wob")
        tl_wk, tl_wv, tl_wq, tl_wz, tl_wob = (cx.new_tl(n) for n in ("tl_bwk", "tl_bwv", "tl_bwq", "tl_bwz", "tl_bwob"))
        tlB_w = (tl_wk, tl_wv, tl_wq, tl_wz, tl_wob)
        w_kv = w_in[:, :, 2048:4096]
        w_qz = w_in[:, :, 0:2048]
        w_ob = w_in[:, :, 4096:5120]
        KT = w_in[:, :, 5120:6144].rearrange("p k (s t) -> p k s t", s=4)
        w_kv_v = w_kv_d.rearrange("(k p) n -> p k n", p=128)
        w_qz_v = w_qz_d.rearrange("(k p) n -> p k n", p=128)
        w_ob_v = w_ob_d.rearrange("(k p) n -> p k n", p=128)

        def rms_rstd(xT, xT_b, sq, sq_b, rs, rs_b, rstd, rstd_b, part=0, pool_sq=True):
            if part in (0, 1):
                for k in range(8):
                    if POOL_SQ and pool_sq:
                        cx.op(POOL, lambda k=k: nc.gpsimd.tensor_tensor(out=sq[:, k, :], in0=xT[:, k, :], in1=xT[:, k, :],
                                                                       op=ALU.mult),
                              reads=[xT_b[k]], writes=[sq_b[k]])
                    else:
                        cx.op(ACT, lambda k=k: nc.scalar.activation(out=sq[:, k, :], in_=xT[:, k, :], func=AF.Square),
                              reads=[xT_b[k]], writes=[sq_b[k]])
            if part in (0, 2):
                bi, bank, bb = newbank()

                def mm():
                    ins = None
                    for k in range(8):
                        ins = nc.tensor.matmul(bank[:, 0:TT], ones_bf[:], sq[:, k, :], start=(k == 0), stop=(k == 7))
                    return ins
                cx.op(PE, mm, reads=[b_ones] + sq_b, writes=[bb])
                cx.op(ACT, lambda: nc.scalar.activation(out=rs[:], in_=bank[:, 0:TT], func=AF.Ln,
                                                        bias=eps_t[:], scale=1.0 / D),
                      reads=[bb, b_eps], writes=[rs_b])
                cx.op(ACT, lambda: nc.scalar.activation(out=rstd[:], in_=rs[:], func=AF.Exp, scale=-0.5),
                      reads=[rs_b], writes=[rstd_b])

        with ExitStack() as esA:
            def sb(name, shape, dt):
                return esA.enter_context(nc.sbuf_tensor(name, list(shape), dt))

            w_oa = sb("w_oa_sb", [128, 16, 1024], BF16)
            wsT_bf = sb("wsT_bf", [128, 16, 128], BF16)
            biasH = sb("biasH", [128, 16, 128], F32)
            ga_t = sb("ga_t", [128, 8], F32)
            lng_t = sb("lng_t", [128, 16], F32)

            b_wvA, tl_wvA = Buf("w_v"), cx.new_tl("tl_wvA")
            b_wu = [Buf(f"w_u{g}") for g in range(4)]
            b_wzA = [Buf(f"w_z{g}") for g in range(4)]
            tl_wu = [cx.new_tl(f"tl_wu{g}") for g in range(4)]
            tl_wzA = [cx.new_tl(f"tl_wzA{g}") for g in range(4)]
            b_woa = [Buf(f"w_oa{j}") for j in range(4)]
            tl_woa = [cx.new_tl(f"tl_woa{j}") for j in range(4)]
            b_wsT, b_biasH, b_ga, b_lng = Buf("wsT"), Buf("biasH"), Buf("ga"), Buf("lng")
            tl_small = cx.new_tl("tl_smallA")

            cx.dma(SP, ga_t[:], g_a[:, :], tl_small, writes=[b_ga])
            cx.dma(SP, lng_t[:], lng_d[:, :], tl_small, writes=[b_lng])

            w_in_v = w_in_d.rearrange("(k p) n -> p k n", p=128)
            w_oa_v = w_oa_d.rearrange("(k p) n -> p k n", p=128)
            for k in range(8):
                cx.dma(POOL, w_in[:, k, 2048:4096], w_in_v[:, k, 2048:4096], tl_wvA, writes=[b_wvA])
            if FINE_W:
                for g in range(2):
                    for (c0, tls_, bs_) in ((1024 * g, tl_wu, b_wu), (4096 + 1024 * g, tl_wzA, b_wzA)):
                        for k in range(8):
                            cx.dma(POOL, w_in[:, k, c0:c0 + 1024], w_in_v[:, k, c0:c0 + 1024], tls_[2 * g],
                                   writes=bs_[2 * g:2 * g + 2])
            else:
                for (c0, tls_, bs_) in ((0, tl_wu, b_wu), (4096, tl_wzA, b_wzA)):
                    for k in range(8):
                        cx.dma(POOL, w_in[:, k, c0:c0 + 2048], w_in_v[:, k, c0:c0 + 2048], tls_[0], writes=bs_)

            with ExitStack() as esS:
                wsT_f = esS.enter_context(nc.sbuf_tensor("wsT_f", [128, 16, 128], F32))
                lnb_t = esS.enter_context(nc.sbuf_tensor("lnb_t", [128, 2048], F32))
                bs_t = esS.enter_context(nc.sbuf_tensor("bs_t", [1, 2048], F32))
                ones_f = esS.enter_context(nc.sbuf_tensor("ones_f", [1, 128], F32))
                b_wsf, b_lnb, b_bs, b_onesf = Buf("wsT_f"), Buf("lnb"), Buf("bs"), Buf("ones_f")
                cx.dma(SP, wsT_f[:], wsT_d[:, :, :], tl_small, writes=[b_wsf])
                cx.dma(SP, lnb_t[:], lnb_d[:, :], tl_small, writes=[b_lnb])
                cx.dma(SP, bs_t[:], bs_d[:, :], tl_small, writes=[b_bs])
                cx.seal(tl_small, [b_ga, b_lng, b_wsf, b_lnb, b_bs])
                cx.op(DVE, lambda: nc.vector.memset(ones_f[:], 1.0), writes=[b_onesf])
                cx.op(DVE, lambda: nc.vector.memset(wsT_f[64:128, :, 0:64], 0.0), writes=[b_wsf])
                cx.op(DVE, lambda: nc.vector.tensor_copy(out=wsT_bf[:], in_=wsT_f[:]), reads=[b_wsf], writes=[b_wsT])
                cx.op(DVE, lambda: nc.vector.tensor_scalar(out=lng_t[:], in0=lng_t[:], scalar1=0.5, scalar2=None,
                                                           op0=ALU.mult), reads=[b_lng], writes=[b_lng])
                for g in range(4):
                    bi, bank, bb = newbank()

                    def mm(g=g, bank=bank):
                        ins = None
                        for j in range(4):
                            h = g * 4 + j
                            nc.tensor.matmul(bank[:, j * 128:(j + 1) * 128], lnb_t[:, h * 128:(h + 1) * 128],
                                             wsT_f[:, h, :], start=True, stop=False)
                            ins = nc.tensor.matmul(bank[:, j * 128:(j + 1) * 128], ones_f[:, :],
                                                   bs_t[:, h * 128:(h + 1) * 128], start=False, stop=True)
                        return ins
                    cx.op(PE, mm, reads=[b_lnb, b_wsf, b_onesf, b_bs], writes=[bb])
                    cx.op(DVE, lambda g=g, bank=bank: nc.vector.tensor_scalar(
                        out=biasH[:, g * 4:(g + 1) * 4, :], in0=bank[:, :].rearrange("p (j t) -> p j t", j=4),
                        scalar1=0.5, scalar2=None, op0=ALU.mult),
                        reads=[bb], writes=[b_biasH])
                cx.barrier(exclude=([tl_wvA] + tl_wu + tl_wzA) if EXCL_SETUP_BARRIER else ())

            for j in range(4):
                for k in range(4 * j, 4 * j + 4):
                    cx.dma(POOL, w_oa[:, k, :], w_oa_v[:, k, :], tl_woa[j], writes=[b_woa[j]])

            xT = [sb(f"xT{i}", [128, 8, TT], F32) for i in range(2)]
            hT = [sb(f"hT{i}", [128, 8, TT], BF16) for i in range(2)]
            sq = sb("sq", [128, 8, TT], BF16)
            rs = sb("rs", [128, TT], F32)
            rstd = sb("rstd", [128, TT], F32)
            gv = sb("gv", [128, 2048], F32)
            nrm = sb("nrm", [128, 2048], BF16)
            stats = sb("stats", [128, 4, 6], F32)
            mv = sb("mv", [128, 2], F32)
            ve = sb("ve", [128, 1], F32)
            rsl = sb("rsl", [128, 1], F32)
            nb = sb("nb", [128, 1], F32)
            mixT = sb("mixT", [128, 16, TT], BF16)
            NGB = 3
            gu = [sb(f"gu{i}", [128, TT], F32) for i in range(NGB)]
            th = [sb(f"th{i}", [128, TT], F32) for i in range(NGB)]
            yT = sb("yT", [128, 16, TT], BF16)

            xT_b = [[Buf(f"xT{i}_{k}") for k in range(8)] for i in range(2)]
            hT_b = [[Buf(f"hT{i}_{k}") for k in range(8)] for i in range(2)]
            sq_b = [Buf(f"sq{k}") for k in range(8)]
            rs_b, rstd_b = Buf("rs"), Buf("rstd")
            gv_b = [Buf(f"gv{f}") for f in range(4)]
            nrm_b = Buf("nrm")
            stats_b = [Buf(f"stats{f}") for f in range(4)]
            mv_b, ve_b, rsl_b, nb_b = Buf("mv"), Buf("ve"), Buf("rsl"), Buf("nb")
            mix_b = [[Buf(f"mix{h}_{t}") for t in range(2)] for h in range(16)]
            gu_b = [Buf(f"gu{i}") for i in range(NGB)]
            th_b = [Buf(f"th{i}") for i in range(NGB)]
            yT_b = [Buf(f"yT{h}") for h in range(16)]
            tl_x = [cx.new_tl(f"tl_xload{i}") for i in range(2)]
            tl_x1st = [cx.new_tl(f"tl_x1store{i}") for i in range(2)]
            uz_bank = {}

            def A_load(ti):
                s = ti % 2
                cx.dma(SP, xT[s][:], x_v[:, :, ti * TT:(ti + 1) * TT], tl_x[s], writes=xT_b[s])

            def A_front(ti, part):
                s = ti % 2
                rms_rstd(xT[s], xT_b[s], sq, sq_b, rs, rs_b, rstd, rstd_b, part=part)
                if part in (0, 2):
                    for k in range(8):
                        cx.op(DVE, lambda k=k: nc.vector.scalar_tensor_tensor(
                            out=hT[s][:, k, :], in0=xT[s][:, k, :], scalar=ga_t[:, k:k + 1], in1=rstd[:],
                            op0=ALU.mult, op1=ALU.mult),
                            reads=[xT_b[s][k], b_ga, rstd_b], writes=[hT_b[s][k]])

            def A_v(ti, tt):
                s = ti % 2
                for fc in range(4):
                    bi, bank, bb = newbank()

                    def mm(fc=fc, bank=bank):
                        ins = None
                        for k in range(8):
                            ins = nc.tensor.matmul(bank[:, :], hT[s][:, k, tt * 128:(tt + 1) * 128],
                                                   w_in[:, k, 2048 + fc * 512:2048 + (fc + 1) * 512],
                                                   start=(k == 0), stop=(k == 7))
                        return ins
                    cx.op(PE, mm, reads=hT_b[s] + [b_wvA], writes=[bb])
                    cx.op(ACT, lambda fc=fc, bank=bank: nc.scalar.activation(
                        out=gv[:, fc * 512:(fc + 1) * 512], in_=bank[:, :], func=AF.Gelu_apprx_tanh),
                        reads=[bb], writes=[gv_b[fc]])
                    cx.op(DVE, lambda fc=fc: nc.vector.bn_stats(out=stats[:, fc, :], in_=gv[:, fc * 512:(fc + 1) * 512]),
                          reads=[gv_b[fc]], writes=[stats_b[fc]])
                cx.op(DVE, lambda: nc.vector.bn_aggr(out=mv[:], in_=stats[:, :, :].rearrange("p a b -> p (a b)")),
                      reads=stats_b, writes=[mv_b])
                cx.op(DVE, lambda: nc.vector.tensor_scalar(out=ve[:], in0=mv[:, 1:2], scalar1=EPS, scalar2=None, op0=ALU.add),
                      reads=[mv_b], writes=[ve_b])
                cx.op(POOL, lambda: nc.gpsimd.tensor_tensor(out=rsl[:], in0=ve[:], in1=mhalf_t[:], op=ALU.pow),
                      reads=[ve_b, b_mhalf], writes=[rsl_b])
                cx.op(DVE, lambda: nc.vector.tensor_scalar(out=nb[:], in0=mv[:, 0:1], scalar1=rsl[:, 0:1], scalar2=-1.0,
                                                           op0=ALU.mult, op1=ALU.mult),
                      reads=[mv_b, rsl_b], writes=[nb_b])

            def A_nrm(ti, tt):
                cx.op(ACT, lambda: nc.scalar.activation(out=nrm[:], in_=gv[:], func=AF.Identity,
                                                        bias=nb[:], scale=rsl[:]),
                      reads=gv_b + [nb_b, rsl_b], writes=[nrm_b])

            def A_spatial(ti, tt):
                for g in range(4):
                    bi, bank, bb = newbank()

                    def mm(g=g, bank=bank):
                        ins = None
                        for j in range(4):
                            h = g * 4 + j
                            ins = nc.tensor.matmul(bank[:, j * 128:(j + 1) * 128], nrm[:, h * 128:(h + 1) * 128],
                                                   wsT_bf[:, h, :], start=True, stop=True)
                        return ins
                    cx.op(PE, mm, reads=[nrm_b, b_wsT], writes=[bb])
                    for j in range(4):
                        h = g * 4 + j
                        cx.op(DVE, lambda h=h, j=j, bank=bank: nc.vector.scalar_tensor_tensor(
                            out=mixT[:, h, tt * 128:(tt + 1) * 128], in0=bank[:, j * 128:(j + 1) * 128],
                            scalar=lng_t[:, h:h + 1], in1=biasH[:, h, :], op0=ALU.mult, op1=ALU.add),
                            reads=[bb, b_lng, b_biasH], writes=[mix_b[h][tt]])

            def A_uz_mm(ti, h):
                s = ti % 2
                bi, bank, bb = newbank(pin=True)
                uz_bank[h] = (bank, bb, bi)

                def mm():
                    ins = None
                    for k in range(8):
                        nc.tensor.matmul(bank[:, 0:TT], w_in[:, k, h * 128:(h + 1) * 128], hT[s][:, k, :],
                                         start=(k == 0), stop=(k == 7))
                    for k in range(8):
                        ins = nc.tensor.matmul(bank[:, TT:2 * TT], w_in[:, k, 4096 + h * 128:4096 + (h + 1) * 128],
                                               hT[s][:, k, :], start=(k == 0), stop=(k == 7))
                    return ins
                cx.op(PE, mm, reads=hT_b[s] + [b_wu[h // 4], b_wzA[h // 4]], writes=[bb])
                i = h % NGB
                cx.op(ACT, lambda: nc.scalar.activation(out=gu[i][:], in_=bank[:, 0:TT], func=AF.Gelu_apprx_tanh),
                      reads=[bb], writes=[gu_b[i]])
                cx.op(ACT, lambda: nc.scalar.activation(out=th[i][:], in_=bank[:, TT:2 * TT], func=AF.Tanh, scale=0.5),
                      reads=[bb], writes=[th_b[i]])

            def A_uz_y(ti, h):
                bank, bb, bi = uz_bank[h]
                cx.unpin(bi)
                i = h % NGB
                cx.op(DVE, lambda: nc.vector.scalar_tensor_tensor(out=th[i][:], in0=th[i][:], scalar=1.0,
                                                                  in1=bank[:, TT:2 * TT], op0=ALU.add, op1=ALU.mult),
                      reads=[th_b[i], bb], writes=[th_b[i]])
                if POOL_Y:
                    cx.op(POOL, lambda: nc.gpsimd.tensor_tensor(out=gu[i][:], in0=gu[i][:], in1=mixT[:, h, :], op=ALU.mult),
                          reads=[gu_b[i]] + mix_b[h], writes=[gu_b[i]])
                    cx.op(DVE, lambda: nc.vector.tensor_tensor(out=yT[:, h, :], in0=th[i][:], in1=gu[i][:], op=ALU.mult),
                          reads=[th_b[i], gu_b[i]], writes=[yT_b[h]])
                else:
                    cx.op(DVE, lambda: nc.vector.tensor_tensor(out=th[i][:], in0=th[i][:], in1=gu[i][:], op=ALU.mult),
                          reads=[th_b[i], gu_b[i]], writes=[th_b[i]])
                    cx.op(DVE, lambda: nc.vector.tensor_tensor(out=yT[:, h, :], in0=th[i][:], in1=mixT[:, h, :], op=ALU.mult),
                          reads=[th_b[i]] + mix_b[h], writes=[yT_b[h]])

            def A_out(ti):
                s = ti % 2
                for db in range(8):
                    bi, bank, bb = newbank()

                    def mm(db=db, bank=bank):
                        ins = None
                        for kf in range(16):
                            ins = nc.tensor.matmul(bank[:, 0:TT], w_oa[:, kf, db * 128:(db + 1) * 128], yT[:, kf, :],
                                                   start=(kf == 0), stop=(kf == 15))
                        return ins
                    cx.op(PE, mm, reads=yT_b + b_woa, writes=[bb])
                    cx.op(DVE, lambda db=db, bank=bank: nc.vector.tensor_tensor(
                        out=xT[s][:, db, :], in0=bank[:, 0:TT], in1=xT[s][:, db, :], op=ALU.add),
                        reads=[bb, xT_b[s][db]], writes=[xT_b[s][db]])
                cx.dma(SP, x1_v[:, :, ti * TT:(ti + 1) * TT], xT[s][:], tl_x1st[s], reads=xT_b[s], writes=[x1_bufs[ti]])

            A_load(0)
            A_load(1)
            A_front(0, 0)
            A_v(0, 0)
            for ti in range(NTT):
                more = ti + 1 < NTT
                A_nrm(ti, 0)
                A_uz_mm(ti, 0)
                A_spatial(ti, 0)
                A_v(ti, 1)
                A_nrm(ti, 1)
                if not more and PREFETCH_B:
                    for (c0, tl, b) in ((0, tl_wk, b_wk), (1024, tl_wv, b_wv)):
                        for k in range(8):
                            cx.dma(POOL, w_kv[:, k, c0:c0 + 1024], w_kv_v[:, k, c0:c0 + 1024], tl, writes=[b], war=[b_wvA])
                A_uz_mm(ti, 1)
                A_uz_mm(ti, 2)
                A_spatial(ti, 1)
                A_uz_y(ti, 0)
                A_uz_y(ti, 1)
                A_uz_y(ti, 2)
                for h in range(3, 16):
                    A_uz_mm(ti, h)
                    A_uz_y(ti, h)
                    if more and h == 5:
                        A_front(ti + 1, 1)
                    if more and h == 9:
                        A_front(ti + 1, 2)
                if not more and PREFETCH_B:
                    for (c0, tl, b) in ((0, tl_wq, b_wq), (1024, tl_wz, b_wz)):
                        for k in range(8):
                            cx.dma(POOL, w_qz[:, k, c0:c0 + 1024], w_qz_v[:, k, c0:c0 + 1024], tl, writes=[b], war=b_wu)
                    for k in range(8):
                        cx.dma(POOL, w_ob[:, k, :], w_ob_v[:, k, :], tl_wob, writes=[b_wob], war=b_wzA)
                if more:
                    A_v(ti + 1, 0)
                A_out(ti)
                if ti + 2 < NTT:
                    A_load(ti + 2)

            cx.barrier(exclude=tlB_w)

        if mode == "A":
            return nc

        with ExitStack() as esB:
            def sb(name, shape, dt):
                return esB.enter_context(nc.sbuf_tensor(name, list(shape), dt))

            Mt = sb("Mt", [128, 16, NDQ], BF16)
            gkv_t = sb("gkv_t", [128, 8], F32)
            gb_t = sb("gb_t", [128, 8], F32)
            gf_t = sb("gf_t", [128, 8], F32)

            b_M = [Buf(f"M{h}") for h in range(16)]
            b_mask, b_gkv, b_gb, b_gf = Buf("mask"), Buf("gkv"), Buf("gb"), Buf("gf")
            tl_smallB = cx.new_tl("tl_smallB")

            if not PREFETCH_B:
                for (c0, tl, b) in ((0, tl_wk, b_wk), (1024, tl_wv, b_wv)):
                    for k in range(8):
                        cx.dma(POOL, w_kv[:, k, c0:c0 + 1024], w_kv_v[:, k, c0:c0 + 1024], tl, writes=[b])
                for (c0, tl, b) in ((0, tl_wq, b_wq), (1024, tl_wz, b_wz)):
                    for k in range(8):
                        cx.dma(POOL, w_qz[:, k, c0:c0 + 1024], w_qz_v[:, k, c0:c0 + 1024], tl, writes=[b])
                for k in range(8):
                    cx.dma(POOL, w_ob[:, k, :], w_ob_v[:, k, :], tl_wob, writes=[b_wob])
            mask_t = sb("mask_t", [128, NDQ], F32)
            NMS = 2
            Mst = [sb(f"Mst{i}", [128, NDQ], F32) for i in range(NMS)]
            Mst_b = [Buf(f"Mst{i}") for i in range(NMS)]
            tl_Mst = [cx.new_tl(f"tl_Mst{i}") for i in range(NMS)]
            cx.dma(SP, gkv_t[:], g_kv[:, :], tl_smallB, writes=[b_gkv])
            cx.dma(SP, gb_t[:], g_b[:, :], tl_smallB, writes=[b_gb])
            cx.dma(SP, gf_t[:], g_f[:, :], tl_smallB, writes=[b_gf])
            cx.dma(SP, mask_t[:], maskT_d[:, :], tl_smallB, writes=[b_mask])
            cx.seal(tl_smallB, [b_gkv, b_gb, b_gf, b_mask])

            def B_setup_M(h0, h1):
                for h in range(h0, h1):
                    i = h % NMS
                    cx.dma(SP, Mst[i][:], biasT_d[:, h * NDQ:(h + 1) * NDQ], tl_Mst[i], writes=[Mst_b[i]])
                    cx.op(ACT, lambda h=h, i=i: nc.scalar.activation(out=Mst[i][:], in_=Mst[i][:], func=AF.Exp),
                          reads=[Mst_b[i]], writes=[Mst_b[i]])
                    cx.op(DVE, lambda h=h, i=i: nc.vector.tensor_tensor(out=Mt[:, h, :], in0=Mst[i][:], in1=mask_t[:],
                                                                       op=ALU.mult),
                          reads=[Mst_b[i], b_mask], writes=[b_M[h]])

            xTs = [sb(f"xTb{i}", [128, 8, TT], F32) for i in range(2)]
            sq = sb("sqb", [128, 8, TT], BF16)
            rs = sb("rsb", [128, TT], F32)
            rstd = sb("rstdb", [128, TT], F32)
            xk = sb("xk", [128, 8, TT], BF16)
            xb = sb("xb", [128, 8, TT], BF16)
            Vr = sb("Vr", [128, 4, 2, 8, 192], BF16)
            QTp = sb("QTp", [128, 8, 2, TT], BF16)
            szT = sb("szT", [128, 8, TT], BF16)
            NSB = 1
            stmp = [sb(f"stmp{i}", [128, 2 * TT], F32) for i in range(NSB)]
            NEB = 4
            Eb = [sb(f"Eb{i}", [128, 2 * TT], BF16) for i in range(NEB)]
            Pb = [sb(f"Pb{i}", [128, 2 * TT], BF16) for i in range(NEB)]
            for i in range(NMS):
                Eb.append(Mst[i][:, 0:TT].bitcast(BF16))
                Pb.append(Mst[i][:, TT:2 * TT].bitcast(BF16))
            Eb.append(mask_t[:, 0:TT].bitcast(BF16))
            Pb.append(mask_t[:, TT:2 * TT].bitcast(BF16))
            NEB_ALL = NEB + NMS + 1
            rd = sb("rd", [128, TT], F32)
            rd2 = sb("rd2", [128, TT], F32)
            ogT = sb("ogT", [128, 8, TT], BF16)

            xTs_b = [[Buf(f"bxT{i}_{k}") for k in range(8)] for i in range(2)]
            sq_b = [Buf(f"bsq{k}") for k in range(8)]
            rs_b, rstd_b = Buf("brs"), Buf("brstd")
            xk_b = [Buf(f"xk{k}") for k in range(8)]
            xb_b = [Buf(f"xb{k}") for k in range(8)]
            KT_b = [[Buf(f"KT{s}_{p}") for p in range(8)] for s in range(4)]
            V_b = [[[Buf(f"V{s}_{t}_{f}") for f in range(2)] for t in range(2)] for s in range(4)]
            QT_b = [Buf(f"QT{p}") for p in range(8)]
            szT_b = [Buf(f"szT{p}") for p in range(8)]
            stmp_b = [Buf(f"stmp{i}") for i in range(NSB)]
            E_b = [Buf(f"E{i}") for i in range(NEB + NMS + 1)]
            P_b = [Buf(f"P{i}") for i in range(NEB + NMS + 1)]
            rd_b, rd2_b = Buf("rd"), Buf("rd2")
            og_b = [Buf(f"og{p}") for p in range(8)]
            tl_xb = [cx.new_tl(f"tl_x1load{i}") for i in range(2)]
            tl_out = [cx.new_tl(f"tl_outstore{i}") for i in range(2)]
            ecount = [0]

            for s in range(4):
                for t in range(2):
                    cx.op(DVE, lambda s=s, t=t: nc.vector.memset(Vr[:, s, t, :, 64:128], 1.0),
                          writes=[V_b[s][t][0], V_b[s][t][1]])
            cx.op(DVE, lambda: nc.vector.memset(QTp[64:128, :, 0, :], 0.0), writes=QT_b)
            cx.op(DVE, lambda: nc.vector.memset(QTp[0:64, :, 1, :], 0.0), writes=QT_b)

            def B_load(tj):
                sj = tj % 2
                cx.dma(SP, xTs[sj][:], x1_v[:, :, tj * TT:(tj + 1) * TT], tl_xb[sj], reads=[x1_bufs[tj]], writes=xTs_b[sj])

            def B_front(tj, pool_sq=True, part=0):
                sj = tj % 2
                xTj, xTj_b = xTs[sj], xTs_b[sj]
                rms_rstd(xTj, xTj_b, sq, sq_b, rs, rs_b, rstd, rstd_b, pool_sq=pool_sq, part=part)
                if part == 1:
                    return
                for k in range(8):
                    cx.op(DVE, lambda k=k: nc.vector.scalar_tensor_tensor(
                        out=xk[:, k, :], in0=xTj[:, k, :], scalar=gkv_t[:, k:k + 1], in1=rstd[:],
                        op0=ALU.mult, op1=ALU.mult),
                        reads=[xTj_b[k], b_gkv, rstd_b], writes=[xk_b[k]])
                if tj >= NHALO_TT:
                    for k in range(8):
                        cx.op(DVE, lambda k=k: nc.vector.scalar_tensor_tensor(
                            out=xb[:, k, :], in0=xTj[:, k, :], scalar=gb_t[:, k:k + 1], in1=rstd[:],
                            op0=ALU.mult, op1=ALU.mult),
                            reads=[xTj_b[k], b_gb, rstd_b], writes=[xb_b[k]])

            def B_K(ti, pp):
                slot = ti % 4
                bi, bank, bb = newbank()

                def mm():
                    ins = None
                    for j in range(2):
                        p = pp * 2 + j
                        for k in range(8):
                            ins = nc.tensor.matmul(bank[:, j * TT:(j + 1) * TT], w_kv[:, k, p * 128:(p + 1) * 128],
                                                   xk[:, k, :], start=(k == 0), stop=(k == 7))
                    return ins
                cx.op(PE, mm, reads=xk_b + [b_wk], writes=[bb])
                cx.op(ACT, lambda: nc.scalar.copy(
                    out=KT[:, 2 * pp:2 * pp + 2, slot, :], in_=bank[:, :].rearrange("p (j t) -> p j t", j=2)),
                    reads=[bb], writes=[KT_b[slot][2 * pp], KT_b[slot][2 * pp + 1]])

            def B_V(ti, tt, fc):
                slot = ti % 4
                bi, bank, bb = newbank()

                def mm():
                    ins = None
                    for k in range(8):
                        ins = nc.tensor.matmul(bank[:, :], xk[:, k, tt * 128:(tt + 1) * 128],
                                               w_kv[:, k, 1024 + fc * 512:1024 + (fc + 1) * 512],
                                               start=(k == 0), stop=(k == 7))
                    return ins
                cx.op(PE, mm, reads=xk_b + [b_wv], writes=[bb])
                bv = bank[:, :].rearrange("p (q c d) -> p q c d", q=4, c=2)
                for c in range(2):
                    cx.op(DVE, lambda c=c: nc.vector.tensor_copy(
                        out=Vr[:, slot, tt, 4 * fc:4 * fc + 4, 128 * c:128 * c + 64], in_=bv[:, :, c, :]),
                        reads=[bb], writes=[V_b[slot][tt][fc]])

            def B_KV(ti):
                for pp in range(4):
                    B_K(ti, pp)
                for tt in range(2):
                    for fc in range(2):
                        B_V(ti, tt, fc)

            def B_QZc(ti, pp):
                bq_i, bankq, bbq = newbank()

                def mmq():
                    ins = None
                    for j in range(2):
                        p = 2 * pp + j
                        for k in range(8):
                            ins = nc.tensor.matmul(bankq[:, j * TT:(j + 1) * TT], w_qz[:, k, p * 128:(p + 1) * 128],
                                                   xb[:, k, :], start=(k == 0), stop=(k == 7))
                    return ins
                cx.op(PE, mmq, reads=xb_b + [b_wq], writes=[bbq])
                qb = [QT_b[2 * pp], QT_b[2 * pp + 1]]
                cx.op(DVE, lambda: nc.vector.tensor_scalar(
                    out=QTp[0:64, 2 * pp:2 * pp + 2, 0, :], in0=bankq[0:64, :].rearrange("p (j t) -> p j t", j=2),
                    scalar1=0.125, scalar2=None, op0=ALU.mult),
                    reads=[bbq], writes=qb)
                cx.op(DVE, lambda: nc.vector.tensor_scalar(
                    out=QTp[64:128, 2 * pp:2 * pp + 2, 1, :], in0=bankq[64:128, :].rearrange("p (j t) -> p j t", j=2),
                    scalar1=0.125, scalar2=None, op0=ALU.mult),
                    reads=[bbq] + qb, writes=qb)
                bz_i, bankz, bbz = newbank()

                def mmz():
                    ins = None
                    for j in range(2):
                        p = 2 * pp + j
                        for k in range(8):
                            ins = nc.tensor.matmul(bankz[:, j * TT:(j + 1) * TT],
                                                   w_qz[:, k, 1024 + p * 128:1024 + (p + 1) * 128],
                                                   xb[:, k, :], start=(k == 0), stop=(k == 7))
                    return ins
                cx.op(PE, mmz, reads=xb_b + [b_wz], writes=[bbz])
                si = pp % NSB
                cx.op(ACT, lambda: nc.scalar.activation(out=stmp[si][:], in_=bankz[:, :], func=AF.Exp, scale=-1.0),
                      reads=[bbz], writes=[stmp_b[si]])
                cx.op(ACT, lambda: nc.scalar.activation(out=stmp[si][:], in_=stmp[si][:], func=AF.Ln, bias=1.0),
                      reads=[stmp_b[si]], writes=[stmp_b[si]])
                cx.op(ACT, lambda: nc.scalar.activation(out=stmp[si][:], in_=stmp[si][:], func=AF.Exp, scale=-1.0),
                      reads=[stmp_b[si]], writes=[stmp_b[si]])
                cx.op(DVE, lambda: nc.vector.tensor_tensor(
                    out=szT[:, 2 * pp:2 * pp + 2, :], in0=bankz[:, :].rearrange("p (j t) -> p j t", j=2),
                    in1=stmp[si][:].rearrange("p (j t) -> p j t", j=2), op=ALU.mult),
                    reads=[bbz, stmp_b[si]], writes=[szT_b[2 * pp], szT_b[2 * pp + 1]])

            def alias_stage_buffers():
                for i in range(NMS):
                    for b in (E_b[NEB + i], P_b[NEB + i]):
                        b.w = Mst_b[i].w
                        b.r = dict(Mst_b[i].r)
                for b in (E_b[NEB + NMS], P_b[NEB + NMS]):
                    b.w = b_mask.w
                    b.r = dict(b_mask.r)

            def B_attn(ti):
                order = [i for i in (2, 3, 4, 1, 0, 5) if ti - 2 + i // 2 >= 0]
                steps = [(p, i) for p in range(8) for i in order]
                state = {}

                def geom(i):
                    q0, q1 = (0, 128) if i == 0 else ((128, 256) if i == 5 else (0, 256))
                    dq0 = 512 - 128 * i + q0
                    tsrc = ti - 2 + i // 2
                    return q0, q1, dq0, tsrc % 4, i % 2, False

                def emit_S(p, i):
                    q0, q1, dq0, ks, half, is_halo = geom(i)
                    n = q1 - q0
                    bi, bank, bb = newbank()
                    cx.op(PE, lambda: nc.tensor.matmul(
                        bank[:, 0:2 * n].rearrange("p (c n) -> p c n", c=2),
                        KT[:, p, ks, half * 128:(half + 1) * 128], QTp[:, p, :, q0:q1], start=True, stop=True),
                        reads=[KT_b[ks][p], QT_b[p]], writes=[bb])
                    e = ecount[0] % NEB_ALL
                    ecount[0] += 1
                    cx.op(ACT, lambda: nc.scalar.activation(out=Eb[e][:, 0:2 * n], in_=bank[:, 0:2 * n], func=AF.Exp),
                          reads=[bb], writes=[E_b[e]])
                    ev = Eb[e][:, 0:2 * n].rearrange("p (c n) -> p c n", c=2)
                    pv = Pb[e][:, 0:2 * n].rearrange("p (c n) -> p c n", c=2)
                    mv_ = Mt[:, 2 * p:2 * p + 2, dq0:dq0 + n]
                    cx.op(DVE, lambda: nc.vector.tensor_tensor(out=pv, in0=ev, in1=mv_, op=ALU.mult),
                          reads=[E_b[e], b_M[2 * p], b_M[2 * p + 1]], writes=[P_b[e]])
                    return e

                def emit_PV(p, i, e, first):
                    q0, q1, dq0, ks, half, is_halo = geom(i)
                    n = q1 - q0
                    bi = state["obank"]
                    bank, bb = cx.banks[bi], cx.bank_bufs[bi]

                    def mm():
                        nc.tensor.matmul(bank[:, q0:q1], Vr[:, ks, half, p, 0:128], Pb[e][:, 0:n],
                                         start=first, stop=False)
                        return nc.tensor.matmul(bank[:, TT + q0:TT + q1], Vr[:, ks, half, p, 64:192], Pb[e][:, n:2 * n],
                                                start=False, stop=(i == 5))
                    cx.op(PE, mm, reads=[V_b[ks][half][p // 4], P_b[e]], writes=[bb], acc=not first)

                def emit_fin1(p, bi):
                    bank, bb = cx.banks[bi], cx.bank_bufs[bi]
                    cx.op(ACT, lambda: nc.scalar.activation(out=rd[0:64, :], in_=bank[64:128, 0:TT], func=AF.Ln),
                          reads=[bb], writes=[rd_b])
                    cx.op(ACT, lambda: nc.scalar.activation(out=rd[64:128, :], in_=bank[0:64, TT:2 * TT], func=AF.Ln),
                          reads=[bb, rd_b], writes=[rd_b])
                    cx.op(ACT, lambda: nc.scalar.activation(out=rd[:], in_=rd[:], func=AF.Exp, scale=-1.0),
                          reads=[rd_b], writes=[rd_b])
                    return (p, bi)

                def emit_fin2(p, bi):
                    bank, bb = cx.banks[bi], cx.bank_bufs[bi]
                    cx.op(DVE, lambda: nc.vector.tensor_tensor(out=rd2[:], in0=rd[:], in1=szT[:, p, :], op=ALU.mult),
                          reads=[rd_b, szT_b[p]], writes=[rd2_b])
                    cx.op(DVE, lambda: nc.vector.tensor_tensor(out=ogT[0:64, p, :], in0=bank[0:64, 0:TT], in1=rd2[0:64, :],
                                                               op=ALU.mult),
                          reads=[bb, rd2_b], writes=[og_b[p]])
                    cx.op(DVE, lambda: nc.vector.tensor_tensor(out=ogT[64:128, p, :], in0=bank[64:128, TT:2 * TT],
                                                               in1=rd2[64:128, :], op=ALU.mult),
                          reads=[bb, rd2_b, og_b[p]], writes=[og_b[p]])
                    cx.unpin(bi)

                fin1_q, fin2_q = [], []
                LOOK = 6
                sched = {}
                nst = len(steps)

                def at(s_, fn_):
                    sched.setdefault((s_ * nst) // 48, []).append(fn_)
                for pp_, s_ in ((1, 2), (2, 8), (3, 13)):
                    f_ = steps.index((2 * pp_, order[0]))
                    if f_ - LOOK - 1 < 0:
                        B_QZc(ti, pp_)
                    else:
                        pos_ = min((s_ * nst) // 48, f_ - LOOK - 1)
                        sched.setdefault(pos_, []).append(lambda pp_=pp_: B_QZc(ti, pp_))
                if ti - 1 >= 0:
                    at(1, lambda: B_final1(ti - 1, part=1))
                    at(4, lambda: B_final1(ti - 1, part=2))
                    at(7, lambda: B_final2(ti - 1))
                    if ti + 1 < NTT:
                        at(7, lambda: B_load(ti + 1))
                if ti + 1 < NTT:
                    at(15, lambda: B_front(ti + 1, part=1))
                    at(19, lambda: B_front(ti + 1, part=2))
                    for j_, s_ in enumerate((21, 24, 27, 30)):
                        at(s_, lambda j_=j_: B_K(ti + 1, j_))
                    for j_, s_ in enumerate((33, 36, 39, 42)):
                        at(s_, lambda j_=j_: B_V(ti + 1, j_ // 2, j_ % 2))
                pend = [emit_S(*steps[j]) for j in range(LOOK)]
                for s, (p, i) in enumerate(steps):
                    for fn_ in sched.pop(s, ()):
                        fn_()
                    if s + LOOK < len(steps):
                        pend.append(emit_S(*steps[s + LOOK]))
                    while fin2_q and fin2_q[0][0] <= s:
                        _, a_ = fin2_q.pop(0)
                        emit_fin2(*a_)
                    while fin1_q and fin1_q[0][0] <= s:
                        if fin2_q:
                            _, a_ = fin2_q.pop(0)
                            emit_fin2(*a_)
                        _, pp_ = fin1_q.pop(0)
                        fin2_q.append((s + 2, emit_fin1(*pp_)))
                    first = (i == order[0])
                    if first:
                        state["obank"] = cx.next_bank(pin=True)
                    emit_PV(p, i, pend.pop(0), first)
                    if i == 5:
                        fin1_q.append((s + 2, (p, state["obank"])))
                for s_ in sorted(sched):
                    for fn_ in sched[s_]:
                        fn_()
                for _, a_ in fin2_q:
                    emit_fin2(*a_)
                for _, pp_ in fin1_q:
                    emit_fin2(*emit_fin1(*pp_))


            def B_out(ti):
                xT, xT_b = xTs[ti % 2], xTs_b[ti % 2]
                for db in range(8):
                    bi, bank, bb = newbank()

                    def mm(db=db, bank=bank):
                        ins = None
                        for kp in range(8):
                            ins = nc.tensor.matmul(bank[:, 0:TT], w_ob[:, kp, db * 128:(db + 1) * 128], ogT[:, kp, :],
                                                   start=(kp == 0), stop=(kp == 7))
                        return ins
                    cx.op(PE, mm, reads=og_b + [b_wob], writes=[bb])
                    cx.op(DVE, lambda db=db, bank=bank: nc.vector.tensor_tensor(
                        out=xT[:, db, :], in0=bank[:, 0:TT], in1=xT[:, db, :], op=ALU.add),
                        reads=[bb, xT_b[db]], writes=[xT_b[db]])

            def B_final1(ti, part=0):
                xT, xT_b = xTs[ti % 2], xTs_b[ti % 2]
                rms_rstd(xT, xT_b, sq, sq_b, rs, rs_b, rstd, rstd_b, part=part)

            def B_final2(ti):
                t0 = ti * TT
                xT, xT_b = xTs[ti % 2], xTs_b[ti % 2]
                for k in range(8):
                    cx.op(DVE, lambda k=k: nc.vector.scalar_tensor_tensor(
                        out=xT[:, k, :], in0=xT[:, k, :], scalar=gf_t[:, k:k + 1], in1=rstd[:],
                        op0=ALU.mult, op1=ALU.mult),
                        reads=[xT_b[k], b_gf, rstd_b], writes=[xT_b[k]])
                o0 = t0
                cx.dma(SP, out_v[:, :, o0:o0 + TT], xT[:], tl_out[ti % 2], reads=xT_b)

            B_load(0)
            B_load(1)
            B_front(0, pool_sq=False)
            B_setup_M(0, 2)
            B_KV(0)
            B_setup_M(2, 8)
            B_QZc(0, 0)
            B_setup_M(8, 16)
            alias_stage_buffers()
            for ti in range(NTT):
                B_attn(ti)
                if ti + 1 < NTT:
                    B_QZc(ti + 1, 0)
                B_out(ti)
            B_final1(NTT - 1)
            B_final2(NTT - 1)

            cx.barrier()
            for t in tl_out:
                SP.wait((t, t.count))
    return nc


def _chunk_cols(v):
    return np.ascontiguousarray(np.asarray(v, dtype=np.float32).reshape(-1, 128).T)


def make_in_maps(x, a_norm_g, a_w_in, a_ln_g, a_ln_b, a_w_s, a_b_s, a_w_out, kv_norm_g, w_kv, b_norm_g,
                 b_w_qz, b_rel_bias, b_w_out, final_norm_g):
    x = np.asarray(x, dtype=np.float32)
    ki = np.arange(128)[:, None]
    dq = np.arange(NDQ)[None, :]
    idx = np.clip(dq - ki, -256, 256) + 256
    rb = np.asarray(b_rel_bias, dtype=np.float32)[0]
    biasT = np.ascontiguousarray(np.transpose(rb[:, idx], (1, 0, 2))).reshape(128, 16 * NDQ)
    cd = dq // 64 - ki // 64
    maskT = ((cd >= 0) & (cd <= 8)).astype(np.float32)
    shared = {
        "g_a": _chunk_cols(a_norm_g[0]),
        "w_in": np.ascontiguousarray(a_w_in[0], dtype=np.float32),
        "lng": _chunk_cols(a_ln_g[0]),
        "lnb": np.ascontiguousarray(np.broadcast_to(np.asarray(a_ln_b[0], dtype=np.float32)[None, :], (128, 2048))),
        "wsT": np.ascontiguousarray(np.transpose(np.asarray(a_w_s[0], dtype=np.float32), (2, 0, 1))),
        "bs": np.ascontiguousarray(np.asarray(a_b_s[0], dtype=np.float32).reshape(1, 2048)),
        "w_oa": np.ascontiguousarray(a_w_out[0], dtype=np.float32),
        "g_kv": _chunk_cols(kv_norm_g),
        "w_kv": np.ascontiguousarray(w_kv, dtype=np.float32),
        "g_b": _chunk_cols(b_norm_g[0]),
        "w_qz": np.ascontiguousarray(b_w_qz[0], dtype=np.float32),
        "biasT": biasT,
        "maskT": maskT,
        "w_ob": np.ascontiguousarray(b_w_out[0], dtype=np.float32),
        "g_f": _chunk_cols(final_norm_g),
    }
    in_maps = []
    for c in range(NCORE):
        b, half = c // 2, c % 2
        lo = 0 if half == 0 else SPLIT - HALO
        m = dict(shared)
        m["x_t"] = np.ascontiguousarray(x[b, lo:lo + NT, :].T)
        in_maps.append(m)
    return in_maps


_NC_CACHE = {}


def kernel(**inputs):
    in_maps = make_in_maps(**inputs)
    if "full" not in _NC_CACHE:
        _NC_CACHE["full"] = build_program("full")
    nc = _NC_CACHE["full"]
    res = run_bass_kernel_spmd(nc, in_maps, core_ids=list(range(NCORE)))
    out = np.empty((4, SEQ, D), dtype=np.float32)
    for c in range(NCORE):
        b, half = c // 2, c % 2
        o = res.results[c]["out_t"]
        if half == 0:
            out[b, 0:SPLIT, :] = o.T
        else:
            out[b, SPLIT:SEQ, :] = o[:, HALO:].T
    return out
```
